# Optimizing a Trainium2 kernel written in Bass

```python
import jax, jax.numpy as jnp
from jax import lax
import numpy as np

D_MODEL = 1024
BATCH = 4
SEQ = 4096
DEPTH = 4

CHUNK = 64
N_BRANCH = 4
BRANCH_W = 256
CONV_W = 3
POOL_WINDOWS = (2, 4, 8, 16)
POOL_GROUPS = len(POOL_WINDOWS)
POOL_GC = BRANCH_W // POOL_GROUPS
RET_HEADS = 4
RET_DK = BRANCH_W // RET_HEADS
RET_DV = BRANCH_W // RET_HEADS
ROPE_BASE = 10000.0
SGU_LEN = 128
SGU_GROUPS = 4
SGU_GC = BRANCH_W // SGU_GROUPS
D_FF = 2816
ALPHA = (2.0 * DEPTH) ** 0.25
BETA = (8.0 * DEPTH) ** -0.25
LN_EPS = 1e-5
GN_EPS = 1e-5

A_COLS = 3 * BRANCH_W
P_COLS = BRANCH_W
R_COLS = 4 * BRANCH_W
S_COLS = 2 * BRANCH_W
IN_COLS = A_COLS + P_COLS + R_COLS + S_COLS

kernel_name = "hybrid_gated_streaming_encoder"


def layer_norm(x, g, b):
    xf = x.astype(jnp.float32)
    mu = jnp.mean(xf, -1, keepdims=True)
    var = jnp.mean(jnp.square(xf - mu), -1, keepdims=True)
    return ((xf - mu) * lax.rsqrt(var + LN_EPS) * g + b).astype(x.dtype)


def swiglu_ffn(x, w1, w2):
    gate, up = jnp.split(x @ w1, 2, axis=-1)
    return (jax.nn.silu(gate) * up) @ w2


def short_conv_mixer(z, conv_w):
    bg, cg, xin = jnp.split(z, 3, axis=-1)
    y = lax.conv_general_dilated(
        cg * xin, conv_w[:, None, :].astype(z.dtype), window_strides=(1,),
        padding=[(CONV_W - 1, 0)], dimension_numbers=('NWC', 'WIO', 'NWC'),
        feature_group_count=BRANCH_W)
    return bg * y


def pool_mixer(z, pool_w, pool_scale):
    B, S, _ = z.shape
    zf = z.astype(jnp.float32).reshape(B, S, POOL_GROUPS, POOL_GC)
    cs = jnp.cumsum(zf, axis=1)
    t = jnp.arange(S)
    outs = []
    for g, w in enumerate(POOL_WINDOWS):
        c = cs[:, :, g]
        prev = jnp.pad(c, ((0, 0), (w, 0), (0, 0)))[:, :S]
        cnt = jnp.minimum(t + 1, w).astype(jnp.float32)[None, :, None]
        outs.append((c - prev) / cnt)
    pooled = jnp.stack(outs, axis=2)
    mixed = (pooled - zf).astype(z.dtype)
    y = jnp.einsum('bsgc,gcd->bsgd', mixed, pool_w)
    return y.reshape(B, S, BRANCH_W) * pool_scale


def rotary(x, pos):
    half = x.shape[-1] // 2
    inv = ROPE_BASE ** (-jnp.arange(half, dtype=jnp.float32) / half)
    ang = pos.astype(jnp.float32)[:, None] * inv[None, :]
    cos = jnp.cos(ang)[None, :, None, :]
    sin = jnp.sin(ang)[None, :, None, :]
    x1, x2 = x[..., :half], x[..., half:]
    return jnp.concatenate([x1 * cos - x2 * sin, x1 * sin + x2 * cos], axis=-1)


def retention_mixer(z, gn_g, gn_b):
    B, S, _ = z.shape
    H, N = RET_HEADS, S // CHUNK
    q, k, v, g = jnp.split(z.astype(jnp.float32), 4, axis=-1)
    pos = jnp.arange(S)
    q = rotary(q.reshape(B, S, H, RET_DK), pos)
    k = rotary(k.reshape(B, S, H, RET_DK), pos) * (RET_DK ** -0.5)
    v = v.reshape(B, S, H, RET_DV)

    def to_chunks(t):
        return t.reshape(B, N, CHUNK, H, -1).transpose(0, 3, 1, 2, 4)

    qc, kc, vc = to_chunks(q), to_chunks(k), to_chunks(v)
    log_gamma = jnp.log1p(-(2.0 ** (-5.0 - jnp.arange(H, dtype=jnp.float32))))
    idx = jnp.arange(CHUNK, dtype=jnp.float32)
    diff = idx[:, None] - idx[None, :]
    decay = jnp.where(diff >= 0,
                      jnp.exp(log_gamma[:, None, None] * jnp.maximum(diff, 0.0)), 0.0)
    scores = jnp.einsum('bhncd,bhnmd->bhncm', qc, kc) * decay[None, :, None]
    inner = jnp.einsum('bhncm,bhnme->bhnce', scores, vc)

    zeta = jnp.exp(log_gamma[:, None] * (CHUNK - 1 - idx)[None, :])
    xi = jnp.exp(log_gamma[:, None] * (idx + 1.0)[None, :])
    chunk_decay = jnp.exp(log_gamma * CHUNK)
    kv = jnp.einsum('bhncd,bhnce->bhnde', kc * zeta[None, :, None, :, None], vc)

    def step(state, kv_n):
        return state * chunk_decay[None, :, None, None] + kv_n, state

    _, prev = lax.scan(step, jnp.zeros((B, H, RET_DK, RET_DV), jnp.float32),
                       kv.transpose(2, 0, 1, 3, 4))
    prev = prev.transpose(1, 2, 0, 3, 4)
    cross = jnp.einsum('bhncd,bhnde->bhnce', qc * xi[None, :, None, :, None], prev)
    o = inner + cross

    mu = jnp.mean(o, -1, keepdims=True)
    var = jnp.mean(jnp.square(o - mu), -1, keepdims=True)
    o = (o - mu) * lax.rsqrt(var + GN_EPS)
    o = o.transpose(0, 2, 3, 1, 4).reshape(B, S, BRANCH_W) * gn_g + gn_b
    return (jax.nn.silu(g) * o).astype(z.dtype)


def spatial_gating_mixer(z, ln_g, ln_b, sgu_w, sgu_b):
    B, S, _ = z.shape
    u, v = jnp.split(jax.nn.gelu(z), 2, axis=-1)
    v = layer_norm(v, ln_g, ln_b)
    vc = v.reshape(B, S // SGU_LEN, SGU_LEN, SGU_GROUPS, SGU_GC)
    i = jnp.arange(SGU_LEN)
    mask = (i[None, :] // CHUNK) <= (i[:, None] // CHUNK)
    w = jnp.where(mask[None], sgu_w, jnp.zeros_like(sgu_w))
    mixed = jnp.einsum('gij,bnjgc->bnigc', w, vc) + sgu_b.T[None, None, :, :, None]
    return u * mixed.reshape(B, S, BRANCH_W)


def hybrid_mixer(h, w_in, conv_w, pool_w, pool_scale, ret_gn_g, ret_gn_b,
                 sgu_ln_g, sgu_ln_b, sgu_w, sgu_b, w_branch, w_gate, b_gate, w_out):
    B, S, D = h.shape
    z = h @ w_in
    za, zp, zr, zs = jnp.split(z, [A_COLS, A_COLS + P_COLS, A_COLS + P_COLS + R_COLS], axis=-1)
    ys = jnp.stack([short_conv_mixer(za, conv_w),
                    pool_mixer(zp, pool_w, pool_scale),
                    retention_mixer(zr, ret_gn_g, ret_gn_b),
                    spatial_gating_mixer(zs, sgu_ln_g, sgu_ln_b, sgu_w, sgu_b)], axis=2)
    branches = jnp.einsum('bsnw,nwd->bsnd', ys, w_branch)
    gates = jax.nn.sigmoid(h @ w_gate + b_gate).reshape(B, S, N_BRANCH, D)
    merged = jnp.sum(gates * branches, axis=2)
    return merged @ w_out


def setup_inputs(seed: int = 0) -> dict:
    key = jax.random.key(seed)
    ks = jax.random.split(key, 32)
    L, D, F, W = DEPTH, D_MODEL, D_FF, BRANCH_W
    f32 = jnp.float32

    def nrm(k, shape, scale):
        return jax.random.normal(k, shape, f32) * scale

    def gain(k, shape):
        return 1.0 + 0.02 * jax.random.normal(k, shape, f32)

    return {
        "x": jax.random.normal(ks[0], (BATCH, SEQ, D), f32),
        "ffn1_w1": nrm(ks[1], (L, D, 2 * F), D ** -0.5),
        "ffn1_w2": nrm(ks[2], (L, F, D), BETA * F ** -0.5),
        "ln1_g": gain(ks[3], (L, D)),
        "ln1_b": nrm(ks[4], (L, D), 0.01),
        "w_in": nrm(ks[5], (L, D, IN_COLS), D ** -0.5),
        "conv_w": nrm(ks[6], (L, CONV_W, W), CONV_W ** -0.5),
        "pool_w": nrm(ks[7], (L, POOL_GROUPS, POOL_GC, POOL_GC), POOL_GC ** -0.5),
        "pool_scale": gain(ks[8], (L, W)),
        "ret_gn_g": gain(ks[9], (L, W)),
        "ret_gn_b": nrm(ks[10], (L, W), 0.01),
        "sgu_ln_g": gain(ks[11], (L, W)),
        "sgu_ln_b": nrm(ks[12], (L, W), 0.01),
        "sgu_w": nrm(ks[13], (L, SGU_GROUPS, SGU_LEN, SGU_LEN), 0.5 * SGU_LEN ** -0.5),
        "sgu_b": gain(ks[14], (L, SGU_GROUPS, SGU_LEN)),
        "w_branch": nrm(ks[15], (L, N_BRANCH, W, D), W ** -0.5),
        "w_gate": nrm(ks[16], (L, D, N_BRANCH * D), D ** -0.5),
        "b_gate": nrm(ks[17], (L, N_BRANCH * D), 0.01),
        "w_out": nrm(ks[18], (L, D, D), BETA * D ** -0.5),
        "ln2_g": gain(ks[19], (L, D)),
        "ln2_b": nrm(ks[20], (L, D), 0.01),
        "ffn2_w1": nrm(ks[21], (L, D, 2 * F), D ** -0.5),
        "ffn2_w2": nrm(ks[22], (L, F, D), BETA * F ** -0.5),
        "ln3_g": gain(ks[23], (L, D)),
        "ln3_b": nrm(ks[24], (L, D), 0.01),
    }


def reference(x, ffn1_w1, ffn1_w2, ln1_g, ln1_b, w_in, conv_w, pool_w, pool_scale,
              ret_gn_g, ret_gn_b, sgu_ln_g, sgu_ln_b, sgu_w, sgu_b, w_branch, w_gate,
              b_gate, w_out, ln2_g, ln2_b, ffn2_w1, ffn2_w2, ln3_g, ln3_b):
    for l in range(DEPTH):
        x = layer_norm(ALPHA * x + 0.5 * swiglu_ffn(x, ffn1_w1[l], ffn1_w2[l]), ln1_g[l], ln1_b[l])
        m = hybrid_mixer(x, w_in[l], conv_w[l], pool_w[l], pool_scale[l], ret_gn_g[l], ret_gn_b[l],
                         sgu_ln_g[l], sgu_ln_b[l], sgu_w[l], sgu_b[l], w_branch[l], w_gate[l],
                         b_gate[l], w_out[l])
        x = layer_norm(ALPHA * x + m, ln2_g[l], ln2_b[l])
        x = layer_norm(ALPHA * x + 0.5 * swiglu_ffn(x, ffn2_w1[l], ffn2_w2[l]), ln3_g[l], ln3_b[l])
    return x
```

```python
import numpy as np
from contextlib import ExitStack
import concourse.bass as bass
import concourse.mybir as mybir
from concourse.bass_utils import run_bass_kernel_spmd

F32 = mybir.dt.float32
BF16 = mybir.dt.bfloat16
AF = mybir.ActivationFunctionType
ALU = mybir.AluOpType
AX = mybir.AxisListType

D = 1024
T = 2048
NG = 2
TG = 1024
DEPTH = 4
FF = 2816
NJ = 22
ALPHA = (2.0 * DEPTH) ** 0.25
LN_EPS = 1e-5
GN_EPS = 1e-5
POOL_W = (2, 4, 8, 16)
XW = 168
NSLOT = 3
SLOT = 4096


class StopBuild(Exception):
    pass


class Op:
    __slots__ = ("eng", "fn", "deps", "signal", "count", "dsem", "inc")

    def __init__(self, eng, fn, dsem=None, inc=16):
        self.eng = eng
        self.fn = fn
        self.deps = set()
        self.signal = False
        self.count = 0
        self.dsem = dsem
        self.inc = inc


class Prog:
    ENGS = ("pe", "act", "dve", "pool", "sp")

    def __init__(self):
        self.ops = []
        self.res = {}
        self.dma_keys = []

    def add(self, eng, fn, reads=(), writes=(), dsem=None, inc=16):
        op = Op(eng, fn, dsem, inc)
        if dsem is not None and dsem not in self.dma_keys:
            self.dma_keys.append(dsem)
        deps = set()
        for k in reads:
            st = self.res.get(k)
            if st is not None and st[0] is not None:
                deps.add(st[0])
        for k in writes:
            st = self.res.get(k)
            if st is not None:
                if st[0] is not None:
                    deps.add(st[0])
                for r in st[1]:
                    deps.add(r)
        for d in deps:
            if d is op:
                continue
            if d.dsem is None and d.eng == eng and eng == "pe":
                continue
            op.deps.add(d)
        for k in reads:
            st = self.res.setdefault(k, [None, []])
            st[1].append(op)
        for k in writes:
            self.res[k] = [op, []]
        self.ops.append(op)
        return op

    def pe(self, fn, reads=(), writes=()):
        return self.add("pe", fn, reads, writes)

    def act(self, fn, reads=(), writes=()):
        return self.add("act", fn, reads, writes)

    def dve(self, fn, reads=(), writes=()):
        return self.add("dve", fn, reads, writes)

    def dma(self, queue, dsem, out, in_, reads=(), writes=()):
        return self.add(queue, lambda e: e.dma_start(out=out, in_=in_), reads, writes, dsem=dsem)

    def emit(self, nc, stack):
        for op in self.ops:
            for d in op.deps:
                d.signal = True
        eng_cnt = {e: 0 for e in self.ENGS}
        dma_cnt = {k: 0 for k in self.dma_keys}
        needs = {}
        for op in self.ops:
            nd = {}
            for d in op.deps:
                if d.dsem is not None:
                    nd[("d", d.dsem)] = dma_cnt[d.dsem]
            needs[id(op)] = nd
            if op.dsem is not None:
                dma_cnt[op.dsem] += op.inc
                op.count = dma_cnt[op.dsem]
            elif op.signal:
                eng_cnt[op.eng] += 1
                op.count = eng_cnt[op.eng]
        sems = {}
        for e in self.ENGS:
            sems[e] = stack.enter_context(nc.semaphore("s_" + e))
        for i, k in enumerate(self.dma_keys):
            sems[("d", k)] = stack.enter_context(nc.semaphore("d%d" % i))
        per_eng = {e: [] for e in self.ENGS}
        for op in self.ops:
            per_eng[op.eng].append(op)

        def run(engname, e):
            known = {}
            for op in per_eng[engname]:
                need = dict(needs[id(op)])
                for d in op.deps:
                    if d.dsem is not None:
                        continue
                    if d.count > need.get(d.eng, 0):
                        need[d.eng] = d.count
                for sk, v in need.items():
                    if known.get(sk, 0) >= v:
                        continue
                    e.wait_ge(sems[sk], v)
                    known[sk] = v
                if op.fn is None:
                    continue
                ins = op.fn(e)
                if op.dsem is not None:
                    ins.then_inc(sems[("d", op.dsem)], op.inc)
                elif op.signal:
                    ins.then_inc(sems[op.eng], 1)

        block = stack.enter_context(nc.Block())

        @block.tensor
        def _(e):
            run("pe", e)

        @block.scalar
        def _(e):
            run("act", e)

        @block.vector
        def _(e):
            run("dve", e)

        @block.gpsimd
        def _(e):
            run("pool", e)

        @block.sync
        def _(e):
            run("sp", e)


def build(depth=DEPTH, dbg=None, stop=None):
    nc = bass.Bass("TRN2", target_bir_lowering=False)
    P = Prog()

    def din(name, shape):
        return nc.dram_tensor(name, list(shape), F32, kind="ExternalInput").ap()

    xT_d = din("xT", [D, T])
    w1_d = [din("ffn1_w1", [depth, D, 2 * FF]), din("ffn2_w1", [depth, D, 2 * FF])]
    w2_d = [din("ffn1_w2", [depth, FF, D]), din("ffn2_w2", [depth, FF, D])]
    win_d = din("w_in_ext", [depth, D, 3072])
    wg_d = din("w_gate", [depth, D, 4 * D])
    wbr_d = din("w_branch", [depth, 4, 256, D])
    wo_d = din("w_out", [depth, D, D])
    lnp_d = din("lnp", [128, depth, 6, 8])
    bg_d = din("b_gate", [128, depth, 32])
    convw_d = din("conv_w", [128, depth, 3, 2])
    poolw_d = din("pool_w", [depth, 4, 64, 64])
    vec256_d = din("vec256", [depth, 5, 256])
    vec256p_d = din("vec256p", [128, depth, 5, 2])
    sguwT_d = din("sgu_wT", [depth, 4, 128, 128])
    sgub_d = din("sgu_b", [depth, 4, 128])
    rope_d = din("rope", [128, 2, T])
    decT_d = din("decT", [128, 4, 128])
    xi_d = din("xi", [128, 2, 128])
    zeta_d = din("zeta8", [128, 256])
    cdec_d = din("cdec", [128, 4])
    sinsc_d = din("sinsc", [128, 2, 16])
    pcm1_d = din("pcm1", [128, 2, 16])
    flag_d = din("flag", [128, 1])
    out_d = nc.dram_tensor("outT", [D, T], F32, kind="ExternalOutput").ap()
    if dbg is not None:
        dbg_d = nc.dram_tensor("dbg", [128, dbg], F32, kind="ExternalOutput").ap()

    st = ExitStack()
    with st:
        def sb(name, shape, dt=F32):
            return st.enter_context(nc.sbuf_tensor("s_" + name, list(shape), dt))

        x = sb("x", [128, 8, T])
        xb = sb("xb", [128, 8, TG], BF16)
        abuf = sb("abuf", [128, NJ, TG], BF16)
        ring = [sb("ring%d" % s, [128, SLOT], BF16) for s in range(NSLOT)]
        pbuf = sb("pbuf", [128, 2, 16 + TG], BF16)
        zpbuf = sb("zpbuf", [128, 2, 16 + TG], BF16)
        vnb = sb("vnb", [128, 2, 256], BF16)
        rope = sb("rope", [128, 2, 512])
        lnrb = sb("lnrb", [128, 1, 512], BF16)
        lnsq = sb("lnsq", [128, 1, 512], BF16)
        scr = [sb("scr%d" % i, [128, 512]) for i in range(4)]
        lnm = sb("lnm", [128, 512]); lnr = sb("lnr", [128, 512]); lnt = lnr
        ident = sb("ident", [128, 128], BF16); identf = scr[1][:, 0:128]; poolf = scr[0][:, 0:256].rearrange("p (c d) -> p c d", c=2)
        onesD = sb("onesD", [128, 128], BF16)
        lnp = sb("lnp", [128, depth, 6, 8])
        bgate = sb("bgate", [128, depth, 32])
        convw = sb("convw", [128, depth, 3, 2])
        vec256p = sb("vec256p", [128, depth, 5, 2])
        diag = sb("diag", [128, 2, 3, 128], BF16)
        poolL = sb("poolL", [128, 20, 128], BF16)
        poolP0 = sb("poolP0", [128, 2, 128], BF16)
        sguw = sb("sguw", [128, 4, 128], BF16)
        sgubT = sb("sgubT", [128, 2, 128])
        gnb = sb("gnb", [128, 4, 256])
        decT = sb("decT", [128, 4, 128]); xi = sb("xi", [128, 2, 128]); zeta = sb("zeta", [128, 256])
        cdec = sb("cdec", [128, 4]); sinsc = sb("sinsc", [128, 2, 16]); pcm1 = sb("pcm1", [128, 2, 16])
        flag = sb("flag", [128, 1])
        stf = sb("stf", [128, 2, 64])
        stb = sb("stb", [128, 8, 2, 64], BF16)
        sinb = sb("sinb", [128, 2, 8, 64], BF16)
        snd = sb("snd", [128, XW]); rcv = sb("rcv", [128, XW])
        gstS = sb("gstS", [128, 8]); gst = sb("gst", [128, 16]); gsq = sb("gsq", [128, 256]); gtmp = sb("gtmp", [128, 256]); gtmp2 = gsq
        smb = sb("smb", [128, 1, 512], BF16)
        yrt = sb("yrt", [128, 1, 256], BF16)
        c16 = sb("c16", [128, 2, 16]); c16b = sb("c16b", [128, 2, 16])
        macc = sb("macc", [128, 512])

        def aview(j0, nj):
            return abuf[:, j0:j0 + nj, :]
        yall = abuf
        QR, QX, KR = 8, 10, 12
        KTZ, VTK, SGT, VNT = 14, 16, 18, 20

        def tokview(j0):
            return abuf[:, j0:j0 + 2, :].rearrange("p a (n c) -> p (a n) c", c=256)

        psum = [st.enter_context(nc.psum_tensor("ps%d" % i, [128, 512], F32)) for i in range(6)]
        psln = [st.enter_context(nc.psum_tensor("psln%d" % i, [128, 512], F32)) for i in range(2)]
        bank_ctr = [0]

        def bank():
            b = bank_ctr[0] % 6
            bank_ctr[0] += 1
            return b

        slot_ctr = [0]

        def wload(parts):
            s = slot_ctr[0] % NSLOT
            slot_ctr[0] += 1
            keys = []
            for i, (vf, src) in enumerate(parts):
                k = ("ws", s, i)
                keys.append(k)
                P.dma("pool", ("w", s), vf(ring[s]), src, writes=[k])
            return s, keys

        scr_ctr = [0]

        def nscr():
            i = scr_ctr[0] % 4
            scr_ctr[0] += 1
            return i

        P.dma("sp", "c0", x[:, :, :], xT_d.rearrange("(k p) t -> p k t", p=128), writes=[("x", k, t) for k in range(8) for t in range(4)])
        P.dma("sp", "c1", lnp[:, :, :, :], lnp_d[:, :, :, :], writes=["lnp"])
        P.dma("sp", "c1", bgate[:, :, :], bg_d[:, :, :], writes=["bgate"])
        P.dma("sp", "c1", convw[:, :, :, :], convw_d[:, :, :, :], writes=["convw"])
        P.dma("sp", "c1", vec256p[:, :, :, :], vec256p_d[:, :, :, :], writes=["vec256p"])
        P.dma("sp", "c1", decT[:, :, :], decT_d[:, :, :], writes=["decT"])
        P.dma("sp", "c1", xi[:, :, :], xi_d[:, :, :], writes=["xi"])
        P.dma("sp", "c1", zeta[:, :], zeta_d[:, :], writes=["zeta"])
        P.dma("sp", "c1", cdec[:, :], cdec_d[:, :], writes=["cdec"])
        P.dma("sp", "c1", sinsc[:, :, :], sinsc_d[:, :, :], writes=["sinsc"])
        P.dma("sp", "c1", pcm1[:, :, :], pcm1_d[:, :, :], writes=["pcm1"])
        P.dma("sp", "c1", flag[:, :], flag_d[:, :], writes=["flag"])
        P.add("pool", lambda e: e.memset(identf, 0.0), writes=[("scr", 1)])
        P.add("pool", lambda e: e.affine_select(out=identf, in_=identf, pattern=[[-1, 128]], compare_op=ALU.not_equal,
                                                fill=1.0, base=0, channel_multiplier=1), reads=[("scr", 1)], writes=[("scr", 1)])
        P.dve(lambda e: e.tensor_copy(out=ident[:], in_=identf), reads=[("scr", 1)], writes=["ident"])
        P.dve(lambda e: e.memset(onesD[:], 1.0 / 1024.0), writes=["onesD"])

        def tks(tt4):
            return slice(tt4 * 512, (tt4 + 1) * 512)

        def cast_xb(g):
            for k in range(8):
                if k % 2 == 0:
                    P.act(lambda e, k=k: e.copy(out=xb[:, k, :], in_=x[:, k, g * TG:(g + 1) * TG]),
                          reads=[("x", k, 2 * g), ("x", k, 2 * g + 1)], writes=[("xb", k)])
                else:
                    P.dve(lambda e, k=k: e.tensor_copy(out=xb[:, k, :], in_=x[:, k, g * TG:(g + 1) * TG]),
                          reads=[("x", k, 2 * g), ("x", k, 2 * g + 1)], writes=[("xb", k)])

        def mm_group(out_ap, pairs, reads, b):
            n = len(pairs)
            for i, (l_ap, r_ap) in enumerate(pairs):
                P.pe(lambda e, l_ap=l_ap, r_ap=r_ap, i=i: e.matmul(out_ap, lhsT=l_ap, rhs=r_ap, start=(i == 0), stop=(i == n - 1)),
                     reads=reads, writes=[("ps", b)])
            tick(1)

        pending = []
        pend_group = [None]

        def tick(n=1):
            for _ in range(n):
                if not pending:
                    return
                try:
                    next(pending[0])
                except StopIteration:
                    pending.pop(0)

        def drain():
            while pending:
                tick()
            pend_group[0] = None

        def stage_begin(g):
            if pending and pend_group[0] == g:
                drain()

        def stage_end_ln(l, which, g):
            drain()
            for tt in range(2):
                pending.append(layer_norm(l, which, g, tt))
            pend_group[0] = g

        def layer_norm(l, which, g, tt):
            tt4 = 2 * g + tt
            ts = tks(tt4)
            eps = LN_EPS / (ALPHA * ALPHA)
            for k in range(8):
                i = 0
                P.act(lambda e, k=k, i=i: e.copy(out=lnrb[:, i, :], in_=x[:, k, ts]), reads=[("x", k, tt4)], writes=[("lnrb", i)])
                P.act(lambda e, k=k, i=i: e.activation(out=lnsq[:, i, :], in_=x[:, k, ts], func=AF.Square), reads=[("x", k, tt4)], writes=[("lnsq", i)])
                yield
                P.pe(lambda e, k=k, i=i: e.matmul(psln[0][:, :], lhsT=onesD[:, :], rhs=lnrb[:, i, :], start=(k == 0), stop=(k == 7)),
                     reads=[("lnrb", i), "onesD"], writes=[("psln", 0)])
                P.pe(lambda e, k=k, i=i: e.matmul(psln[1][:, :], lhsT=onesD[:, :], rhs=lnsq[:, i, :], start=(k == 0), stop=(k == 7)),
                     reads=[("lnsq", i), "onesD"], writes=[("psln", 1)])
                yield
            P.act(lambda e: e.copy(out=lnm[:], in_=psln[0][:, :]), reads=[("psln", 0)], writes=["lnm"])
            P.dve(lambda e: e.tensor_tensor(out=lnr[:], in0=lnm[:], in1=lnm[:], op=ALU.mult), reads=["lnm"], writes=["lnr"])
            P.dve(lambda e: e.tensor_tensor(out=lnr[:], in0=psln[1][:, :], in1=lnr[:], op=ALU.subtract), reads=[("psln", 1), "lnr"], writes=["lnr"])
            yield
            P.act(lambda e: e.activation(out=lnr[:], in_=lnr[:], func=AF.Sqrt, bias=eps, scale=1.0), reads=["lnr"], writes=["lnr"])
            P.dve(lambda e: e.reciprocal(out=lnr[:], in_=lnr[:]), reads=["lnr"], writes=["lnr"])
            yield
            for k in range(8):
                si = nscr()
                P.dve(lambda e, k=k, si=si: e.tensor_tensor(out=scr[si][:], in0=x[:, k, ts], in1=lnm[:], op=ALU.subtract),
                      reads=[("x", k, tt4), "lnm"], writes=[("scr", si)])
                P.dve(lambda e, si=si: e.tensor_tensor(out=scr[si][:], in0=scr[si][:], in1=lnr[:], op=ALU.mult),
                      reads=[("scr", si), "lnr"], writes=[("scr", si)])
                P.act(lambda e, k=k, si=si: e.activation(out=x[:, k, ts], in_=scr[si][:], func=AF.Identity,
                                                         bias=lnp[:, l, 2 * which + 1, k:k + 1], scale=lnp[:, l, 2 * which, k:k + 1]),
                      reads=[("scr", si), "lnp"], writes=[("x", k, tt4)])
                yield

        def ffn(l, f, g):
            stage_begin(g)
            cast_xb(g)
            w1v = w1_d[f][l].rearrange("(k p) c -> p k c", p=128)
            w2v = w2_d[f][l].rearrange("(j p) d -> p j d", p=128)
            xbk = [("xb", k) for k in range(8)]
            for jb in range(NJ // 2):
                j0 = 2 * jb
                s, keys = wload([
                    (lambda r: r[:, :].rearrange("p (k q c) -> p k q c", k=8, q=2)[:, :, 0, :], w1v[:, :, j0 * 128:(j0 + 2) * 128]),
                    (lambda r: r[:, :].rearrange("p (k q c) -> p k q c", k=8, q=2)[:, :, 1, :], w1v[:, :, FF + j0 * 128:FF + (j0 + 2) * 128]),
                ])
                rv = ring[s][:, :].rearrange("p (k q c) -> p k q c", k=8, q=2)
                for jj in range(2):
                    j = j0 + jj
                    for tt in range(2):
                        bg_, bu_ = bank(), bank()
                        mm_group(psum[bg_][:, :], [(rv[:, k, 0, jj * 128:(jj + 1) * 128], xb[:, k, tt * 512:(tt + 1) * 512]) for k in range(8)], keys + xbk, bg_)
                        mm_group(psum[bu_][:, :], [(rv[:, k, 1, jj * 128:(jj + 1) * 128], xb[:, k, tt * 512:(tt + 1) * 512]) for k in range(8)], keys + xbk, bu_)
                        si = nscr()
                        P.act(lambda e, si=si, bg_=bg_: e.activation(out=scr[si][:], in_=psum[bg_][:, :], func=AF.Silu), reads=[("ps", bg_)], writes=[("scr", si)])
                        P.dve(lambda e, si=si, bu_=bu_, j=j, tt=tt: e.tensor_tensor(out=abuf[:, j, tt * 512:(tt + 1) * 512], in0=psum[bu_][:, :], in1=scr[si][:], op=ALU.mult),
                              reads=[("ps", bu_), ("scr", si)], writes=[("AB", j, tt)])
            for dp in range(4):
                sl = []
                for hh in range(2):
                    s, keys = wload([(lambda r: r[:, 0:11 * 256].rearrange("p (j c) -> p j c", j=11), w2v[:, hh * 11:(hh + 1) * 11, dp * 256:(dp + 1) * 256])])
                    sl.append((s, keys))
                for dd in range(2):
                    d = dp * 2 + dd
                    for tt in range(2):
                        tt4 = 2 * g + tt
                        b = bank()
                        pairs, rk = [], []
                        for hh in range(2):
                            s, keys = sl[hh]
                            rv = ring[s][:, 0:11 * 256].rearrange("p (j c) -> p j c", j=11)
                            rk += keys
                            for jj in range(11):
                                pairs.append((rv[:, jj, dd * 128:(dd + 1) * 128], abuf[:, hh * 11 + jj, tt * 512:(tt + 1) * 512]))
                        rk += [("AB", j, tt) for j in range(NJ)]
                        mm_group(psum[b][:, :], pairs, rk, b)
                        P.dve(lambda e, b=b, d=d, tt4=tt4: e.scalar_tensor_tensor(out=x[:, d, tks(tt4)], in0=psum[b][:, :], scalar=0.5 / ALPHA, in1=x[:, d, tks(tt4)],
                                                                              op0=ALU.mult, op1=ALU.add),
                              reads=[("ps", b), ("x", d, tt4)], writes=[("x", d, tt4)])
            stage_end_ln(l, 2 * f, g)

        stf2 = sb("stf2", [128, 2, 64])

        def mixer_setup(l):
            for c in range(2):
                for j in range(3):
                    P.dve(lambda e, c=c, j=j: e.tensor_scalar_mul(out=diag[:, c, j, :], in0=ident[:], scalar1=convw[:, l, j, c:c + 1]),
                          reads=["ident", "convw"], writes=[("diag", c)])
            P.dve(lambda e: e.memset(poolf, 0.0), writes=[("scr", 0)])
            for gq in range(4):
                c, h = gq // 2, gq % 2
                P.dma("sp", "c2p", poolf[h * 64:(h + 1) * 64, c, h * 64:(h + 1) * 64], poolw_d[l, gq], reads=[("scr", 0)], writes=[("poolf", gq)])
            pk = [("poolf", q) for q in range(4)] + [("scr", 0)]
            for c in range(2):
                base = 0 if c == 0 else 4
                nsh = 4 if c == 0 else 16
                for j in range(nsh):
                    for h in range(2):
                        w = POOL_W[2 * c + h]
                        sc = (1.0 / w if j < w else 0.0)
                        hs = slice(h * 64, (h + 1) * 64)
                        if j == 0:
                            P.dve(lambda e, c=c, hs=hs, sc=sc: e.tensor_scalar_mul(out=poolP0[hs, c, :], in0=poolf[hs, c, :], scalar1=sc),
                                  reads=pk, writes=[("poolP", c)])
                        P.dve(lambda e, c=c, j=j, hs=hs, sc=sc, base=base: e.tensor_scalar_mul(out=poolL[hs, base + j, :], in0=poolf[hs, c, :], scalar1=(sc - 1.0 if j == 0 else sc)),
                              reads=pk, writes=[("poolL", c)])
            P.dma("pool", "c3", sguw[:, :, :], sguwT_d[l].rearrange("q j i -> j q i"), writes=["sguw"])
            P.dve(lambda e: e.memset(sguw[64:128, :, 0:64], 0.0), reads=["sguw"], writes=["sguw"])
            for c in range(2):
                for h in range(2):
                    P.dma("sp", "c2s", sgubT[h * 64:(h + 1) * 64, c, :], sgub_d[l, 2 * c + h:2 * c + h + 1, :].partition_broadcast(64), writes=[("sgubT", c, h)])
            for i in range(4):
                P.dma("sp", "c2g", gnb[:, i, :], vec256_d[l, i + 1:i + 2, :].partition_broadcast(128), writes=[("gnb", i)])

        def tbank():
            b = bank()
            return b, psum[b][:, :].bitcast(BF16)

        class MX:
            pass

        def mx_ctx(l):
            m = MX()
            m.xbk = [("xb", k) for k in range(8)]
            winv = win_d[l].rearrange("(k p) c -> p k c", p=128)

            def wcols(c0, ncol):
                s, keys = wload([(lambda r: r[:, 0:8 * ncol].rearrange("p (k c) -> p k c", k=8), winv[:, :, c0:c0 + ncol])])
                return ring[s][:, 0:8 * ncol].rearrange("p (k c) -> p k c", k=8), keys

            def proj_fm(rv, keys, ci, tt, b):
                mm_group(psum[b][:, :], [(rv[:, k, ci * 128:(ci + 1) * 128], xb[:, k, tt * 512:(tt + 1) * 512]) for k in range(8)], keys + m.xbk, b)
            m.wcols, m.proj_fm = wcols, proj_fm
            return m

        def rot_qk(m, g, tts, do_q=True):
            rv_qk, k_qk = m.wcols(1024, 512)
            rv_pp, k_pp = m.wcols(2560, 512)
            for tt in tts:
                tt4 = 2 * g + tt
                P.dma("sp", "rope", rope[:, :, :], rope_d[:, :, tt4 * 512:(tt4 + 1) * 512], writes=["rope"])
                for qk in ((0, 1) if do_q else (1,)):
                    for c in range(2):
                        b1, b2 = bank(), bank()
                        m.proj_fm(rv_qk, k_qk, 2 * qk + c, tt, b1)
                        m.proj_fm(rv_pp, k_pp, 2 * qk + c, tt, b2)
                        s1, s2 = nscr(), nscr()
                        P.dve(lambda e, s1=s1, b1=b1: e.tensor_tensor(out=scr[s1][:], in0=psum[b1][:, :], in1=rope[:, 0, :], op=ALU.mult),
                              reads=[("ps", b1), "rope"], writes=[("scr", s1)])
                        P.dve(lambda e, s2=s2, b2=b2: e.tensor_tensor(out=scr[s2][:], in0=psum[b2][:, :], in1=rope[:, 1, :], op=ALU.mult),
                              reads=[("ps", b2), "rope"], writes=[("scr", s2)])
                        dst = QR if qk == 0 else KR
                        P.dve(lambda e, s1=s1, s2=s2, dst=dst, c=c, tt=tt: e.tensor_tensor(out=abuf[:, dst + c, tt * 512:(tt + 1) * 512], in0=scr[s1][:], in1=scr[s2][:], op=ALU.add),
                              reads=[("scr", s1), ("scr", s2)], writes=[("AB", dst + c, tt)])
                        if qk == 0:
                            P.dve(lambda e, c=c, tt=tt: e.tensor_tensor(
                                out=abuf[:, QX + c, tt * 512:(tt + 1) * 512].rearrange("p (n c) -> p n c", c=128),
                                in0=abuf[:, QR + c, tt * 512:(tt + 1) * 512].rearrange("p (n c) -> p n c", c=128),
                                in1=xi[:, c, :].unsqueeze(1).to_broadcast([128, 4, 128]), op=ALU.mult),
                                reads=[("AB", QR + c, tt), "xi"], writes=[("AB", QX + c, tt)])

        def k_transposes():
            ktz = tokview(KTZ)
            for n in range(8):
                tt = n // 4
                b, tv = tbank()
                for c in range(2):
                    P.pe(lambda e, c=c, n=n, tv=tv: e.transpose(tv[:, c * 128:(c + 1) * 128], abuf[:, KR + c, n * 128:(n + 1) * 128], ident[:]),
                         reads=[("AB", KR + c, tt), "ident"], writes=[("ps", b)])
                P.dve(lambda e, n=n, tv=tv: e.tensor_tensor(out=ktz[:, n, :], in0=tv[:, 0:256], in1=zeta[:], op=ALU.mult), reads=[("ps", b), "zeta"], writes=[("AB", KTZ, n)])

        def kv_scan(stt, skey, store_base=None):
            ktz, vtk = tokview(KTZ), tokview(VTK)
            for n in range(8):
                b1 = bank()
                for c in range(2):
                    P.pe(lambda e, c=c, n=n, b1=b1: e.matmul(psum[b1][:, c * 128:(c + 1) * 128], lhsT=ktz[:, n, c * 128:(c + 1) * 128], rhs=vtk[:, n, c * 128:(c + 1) * 128], start=True, stop=True),
                         reads=[("AB", KTZ, n), ("AB", VTK, n)], writes=[("ps", b1)])
                if store_base is not None:
                    ng = store_base + n
                    P.act(lambda e, ng=ng: e.copy(out=stb[:, ng, :, :], in_=stt[:, :, :]), reads=[skey], writes=[("stb", ng)])
                for c in range(2):
                    for h in range(2):
                        hs = slice(h * 64, (h + 1) * 64)
                        P.dve(lambda e, c=c, h=h, hs=hs, b1=b1: e.scalar_tensor_tensor(out=stt[hs, c, :], in0=stt[hs, c, :], scalar=cdec[hs, c:c + 1],
                                                                                     in1=psum[b1][hs, c * 128 + h * 64:c * 128 + (h + 1) * 64], op0=ALU.mult, op1=ALU.add),
                              reads=[skey, ("ps", b1), "cdec"], writes=[skey])

        def mixer_pre(l):
            g = 1
            stage_begin(g)
            cast_xb(g)
            m = mx_ctx(l)
            P.dve(lambda e: e.memset(snd[:], 0.0), writes=["snd"])
            rv_bg, k_bg = m.wcols(0, 512)
            rv_x, k_x = m.wcols(512, 512)
            for c in range(2):
                b1, b2, b3 = bank(), bank(), bank()
                m.proj_fm(rv_bg, k_bg, 2 + c, 1, b1)
                m.proj_fm(rv_x, k_x, c, 1, b2)
                m.proj_fm(rv_x, k_x, 2 + c, 1, b3)
                si = nscr()
                P.act(lambda e, si=si, b1=b1: e.copy(out=scr[si][:, 0:2], in_=psum[b1][:, 510:512]), reads=[("ps", b1)], writes=[("scr", si)])
                P.dve(lambda e, si=si, b2=b2, c=c: e.tensor_tensor(out=snd[:, 160 + 2 * c:162 + 2 * c], in0=psum[b2][:, 510:512], in1=scr[si][:, 0:2], op=ALU.mult),
                      reads=[("ps", b2), ("scr", si), "snd"], writes=[("snd", 1 + c)])
                P.act(lambda e, b3=b3, c=c: e.copy(out=snd[:, 128 + 16 * c:144 + 16 * c], in_=psum[b3][:, 496:512]), reads=[("ps", b3), "snd"], writes=[("snd", 3 + c)])
            rot_qk(m, g, (0, 1), do_q=False)
            rv_v, k_v = m.wcols(1536, 256)
            vtk = tokview(VTK)
            for n in range(8):
                b1 = bank()
                mm_group(psum[b1][:, 0:256], [(xb[:, k, n * 128:(n + 1) * 128], rv_v[:, k, :]) for k in range(8)], k_v + m.xbk, b1)
                P.act(lambda e, b1=b1, n=n: e.copy(out=vtk[:, n, :], in_=psum[b1][:, 0:256]), reads=[("ps", b1)], writes=[("AB", VTK, n)])
            k_transposes()
            P.dve(lambda e: e.memset(stf2[:], 0.0), writes=["stf2"])
            kv_scan(stf2, "stf2")

        def chk(name):
            if stop == name:
                raise StopBuild()

        def mixer(l, g):
            stage_begin(g)
            cast_xb(g)
            m = mx_ctx(l)
            xbk = m.xbk
            ktz, vtk, sgt = tokview(KTZ), tokview(VTK), tokview(SGT)
            uT = abuf[:, 20:22, :]
            if g == 1:
                for c in range(2):
                    P.dve(lambda e, c=c: e.tensor_copy(out=zpbuf[:, c, 0:16], in_=zpbuf[:, c, TG:16 + TG]),
                          reads=[("zp", c, 1)], writes=[("zph", c)])
                    P.dve(lambda e, c=c: e.tensor_copy(out=pbuf[:, c, 14:16], in_=pbuf[:, c, 16 + TG - 2:16 + TG]),
                          reads=[("pb", c, 1)], writes=[("ph", c)])
            rv_bg, k_bg = m.wcols(0, 512)
            rv_x, k_x = m.wcols(512, 512)
            for c in range(2):
                for tt in range(2):
                    b1, b2 = bank(), bank()
                    m.proj_fm(rv_bg, k_bg, 2 + c, tt, b1)
                    m.proj_fm(rv_x, k_x, c, tt, b2)
                    si = nscr()
                    P.act(lambda e, si=si, b1=b1: e.copy(out=scr[si][:], in_=psum[b1][:, :]), reads=[("ps", b1)], writes=[("scr", si)])
                    P.dve(lambda e, si=si, b2=b2, c=c, tt=tt: e.tensor_tensor(out=pbuf[:, c, 16 + tt * 512:16 + (tt + 1) * 512], in0=psum[b2][:, :], in1=scr[si][:], op=ALU.mult),
                          reads=[("ps", b2), ("scr", si)], writes=[("pb", c, tt)])
            for c in range(2):
                for tt in range(2):
                    b1 = bank()
                    m.proj_fm(rv_x, k_x, 2 + c, tt, b1)
                    P.act(lambda e, b1=b1, c=c, tt=tt: e.copy(out=zpbuf[:, c, 16 + tt * 512:16 + (tt + 1) * 512], in_=psum[b1][:, :]),
                          reads=[("ps", b1)], writes=[("zp", c, tt)])
            rot_qk(m, g, (0, 1), do_q=True)
            rv_vg, k_vg = m.wcols(1536, 512)
            for n in range(8):
                b1 = bank()
                mm_group(psum[b1][:, :], [(xb[:, k, n * 128:(n + 1) * 128], rv_vg[:, k, :]) for k in range(8)], k_vg + xbk, b1)
                P.act(lambda e, b1=b1, n=n: e.copy(out=vtk[:, n, :], in_=psum[b1][:, 0:256]), reads=[("ps", b1)], writes=[("AB", VTK, n)])
                P.act(lambda e, b1=b1, n=n: e.activation(out=sgt[:, n, :], in_=psum[b1][:, 256:512], func=AF.Silu), reads=[("ps", b1)], writes=[("AB", SGT, n)])
            k_transposes()
            if g == 0:
                P.dve(lambda e: e.memset(stf[:], 0.0), writes=["stf"])
            kv_scan(stf, "stf", store_base=0)
            chk("m_front")
            if g == 0:
                for c in range(2):
                    P.dve(lambda e, c=c: e.scalar_tensor_tensor(out=snd[:, 64 * c:64 * (c + 1)], in0=stf[:, c, :], scalar=cdec[:, 2 + c:3 + c], in1=stf2[:, c, :], op0=ALU.mult, op1=ALU.add),
                          reads=["stf", "stf2", "cdec", "snd"], writes=[("snd", 5 + c)])
                xs = nc.dram_tensor("xsnd%d" % l, [128, XW], F32)
                xr = nc.dram_tensor("xrcv%d" % l, [256, XW], F32)
                P.dma("sp", "xs", xs.ap()[:, :], snd[:], reads=["snd"] + [("snd", i) for i in range(1, 7)], writes=[("xsd", l)])
                P.add("pool", lambda e: e.collective_compute("AllGather", ALU.bypass, replica_groups=[[0, 1], [2, 3], [4, 5], [6, 7]],
                                                             ins=[xs.ap().opt()], outs=[xr.ap().opt()]),
                      reads=[("xsd", l)], writes=[("xrd", l)], dsem=("cc", l), inc=1)
                P.dma("sp", "xr", rcv[:], xr.ap()[0:128, :], reads=[("xrd", l)], writes=["rcv"])
            chk("m_xch")
            rv_uv, k_uv = m.wcols(2048, 512)
            for c in range(2):
                for tt in range(2):
                    b1 = bank()
                    m.proj_fm(rv_uv, k_uv, c, tt, b1)
                    P.act(lambda e, b1=b1, c=c, tt=tt: e.activation(out=uT[:, c, tt * 512:(tt + 1) * 512], in_=psum[b1][:, :], func=AF.Gelu_apprx_tanh),
                          reads=[("ps", b1)], writes=[("uT", c, tt)])
            chk("s_u")
            drain()
            gsqS = lnrb[:, 0, :].bitcast(F32)
            gtmpS = lnsq[:, 0, :].bitcast(F32)
            KQ, KT = ("lnrb", 0), ("lnsq", 0)

            def sgu_tile(n):
                b1 = bank()
                mm_group(psum[b1][:, 0:256], [(xb[:, k, n * 128:(n + 1) * 128], rv_uv[:, k, 256:512]) for k in range(8)], k_uv + xbk, b1)
                yield
                P.act(lambda e, b1=b1: e.activation(out=gtmpS, in_=psum[b1][:, 0:256], func=AF.Gelu_apprx_tanh), reads=[("ps", b1)], writes=[KT])
                yield
                P.dve(lambda e: e.reduce_sum(out=gstS[:, 0:1], in_=gtmpS, axis=AX.X), reads=[KT], writes=["gstS"])
                P.act(lambda e: e.activation(out=gsqS, in_=gtmpS, func=AF.Square), reads=[KT], writes=[KQ])
                yield
                P.dve(lambda e: e.reduce_sum(out=gstS[:, 1:2], in_=gsqS, axis=AX.X), reads=[KQ, "gstS"], writes=["gstS"])
                P.dve(lambda e: e.tensor_scalar_mul(out=gstS[:, 2:4], in0=gstS[:, 0:2], scalar1=1.0 / 256.0), reads=["gstS"], writes=["gstS"])
                P.dve(lambda e: e.tensor_tensor(out=gstS[:, 4:5], in0=gstS[:, 2:3], in1=gstS[:, 2:3], op=ALU.mult), reads=["gstS"], writes=["gstS"])
                P.dve(lambda e: e.tensor_tensor(out=gstS[:, 5:6], in0=gstS[:, 3:4], in1=gstS[:, 4:5], op=ALU.subtract), reads=["gstS"], writes=["gstS"])
                yield
                P.act(lambda e: e.activation(out=gstS[:, 6:7], in_=gstS[:, 5:6], func=AF.Sqrt, bias=LN_EPS, scale=1.0), reads=["gstS"], writes=["gstS"])
                yield
                P.dve(lambda e: e.reciprocal(out=gstS[:, 7:8], in_=gstS[:, 6:7]), reads=["gstS"], writes=["gstS"])
                P.dve(lambda e: e.tensor_scalar(out=gsqS, in0=gtmpS, scalar1=gstS[:, 2:3], scalar2=gstS[:, 7:8], op0=ALU.subtract, op1=ALU.mult),
                      reads=[KT, "gstS", KQ], writes=[KQ])
                P.dve(lambda e: e.tensor_tensor(out=gsqS, in0=gsqS, in1=gnb[:, 2, :], op=ALU.mult), reads=[KQ, ("gnb", 2)], writes=[KQ])
                P.dve(lambda e, n=n: e.tensor_tensor(out=vnb[:, n % 2, :], in0=gsqS, in1=gnb[:, 3, :], op=ALU.add), reads=[KQ, ("gnb", 3)], writes=[("vnb", n % 2)])
                yield
                b2 = bank()
                for c in range(2):
                    for h in range(2):
                        P.pe(lambda e, c=c, h=h, n=n, b2=b2: e.matmul(psum[b2][:, (2 * c + h) * 128:(2 * c + h + 1) * 128], lhsT=vnb[:, n % 2, c * 128:(c + 1) * 128], rhs=sguw[:, 2 * c + h, :], start=True, stop=True),
                             reads=[("vnb", n % 2), "sguw"], writes=[("ps", b2)])
                yield
                tt, nn = n // 4, n % 4
                for c in range(2):
                    for h in range(2):
                        hs = slice(h * 64, (h + 1) * 64)
                        P.dve(lambda e, c=c, h=h, hs=hs, b2=b2: e.tensor_tensor(out=gtmpS[hs, c * 128:(c + 1) * 128], in0=psum[b2][hs, (2 * c + h) * 128:(2 * c + h + 1) * 128], in1=sgubT[hs, c, :], op=ALU.add),
                              reads=[("ps", b2), ("sgubT", c, h), KT], writes=[KT])
                    P.dve(lambda e, c=c, n=n: e.tensor_tensor(out=abuf[:, 6 + c, n * 128:(n + 1) * 128], in0=uT[:, c, n * 128:(n + 1) * 128], in1=gtmpS[:, c * 128:(c + 1) * 128], op=ALU.mult),
                          reads=[KT, ("uT", c, tt)], writes=[("AB", 6 + c, tt, nn)])
                yield

            def chain(gens):
                for gg in gens:
                    yield from gg

            n_solo = 4 if g == 0 else 0
            for _ in chain([sgu_tile(n) for n in range(n_solo)]):
                pass
            chk("m_sgu")
            if g == 0:
                for c in range(2):
                    P.dve(lambda e, c=c: e.tensor_scalar_mul(out=zpbuf[:, c, 0:16], in0=rcv[:, 128 + 16 * c:128 + 16 * (c + 1)], scalar1=flag[:, 0:1]),
                          reads=["rcv", "flag"], writes=[("zph", c)])
                    P.dve(lambda e, c=c: e.tensor_scalar_mul(out=pbuf[:, c, 14:16], in0=rcv[:, 160 + 2 * c:160 + 2 * (c + 1)], scalar1=flag[:, 0:1]),
                          reads=["rcv", "flag"], writes=[("ph", c)])

            P.dve(lambda e: e.tensor_tensor(out=sinb[:, :, :, :],
                                            in0=rcv[:, 0:128].rearrange("p (c e) -> p c e", c=2).unsqueeze(2).to_broadcast([128, 2, 8, 64]),
                                            in1=sinsc[:, :, 8 * g:8 * g + 8].unsqueeze(3).to_broadcast([128, 2, 8, 64]), op=ALU.mult),
                  reads=["rcv", "sinsc"], writes=["sinb"])
            rv_b, k_b = m.wcols(0, 256)
            for c in range(2):
                for tt in range(2):
                    b1, b2 = bank(), bank()
                    for j in range(3):
                        P.pe(lambda e, c=c, tt=tt, j=j, b1=b1: e.matmul(psum[b1][:, :], lhsT=diag[:, c, j, :], rhs=pbuf[:, c, 14 + j + tt * 512:14 + j + (tt + 1) * 512], start=(j == 0), stop=(j == 2)),
                             reads=[("diag", c), ("pb", c, tt), ("pb", c, max(tt - 1, 0)), ("ph", c)], writes=[("ps", b1)])
                    m.proj_fm(rv_b, k_b, c, tt, b2)
                    si = nscr()
                    P.act(lambda e, si=si, b2=b2: e.copy(out=scr[si][:], in_=psum[b2][:, :]), reads=[("ps", b2)], writes=[("scr", si)])
                    P.dve(lambda e, si=si, b1=b1, c=c, tt=tt: e.tensor_tensor(out=abuf[:, 0 + c, tt * 512:(tt + 1) * 512], in0=psum[b1][:, :], in1=scr[si][:], op=ALU.mult),
                          reads=[("ps", b1), ("scr", si)], writes=[("AB", 0 + c, tt)])
            chk("m_conv")
            for c in range(2):
                base = 0 if c == 0 else 4
                nsh = 4 if c == 0 else 16
                for tt in range(2):
                    b1 = bank()
                    for j in range(nsh):
                        P.pe(lambda e, c=c, tt=tt, j=j, b1=b1, base=base, nsh=nsh: e.matmul(psum[b1][:, :], lhsT=poolL[:, base + j, :], rhs=zpbuf[:, c, 16 - j + tt * 512:16 - j + (tt + 1) * 512], start=(j == 0), stop=(j == nsh - 1)),
                             reads=[("poolL", c), ("zp", c, tt), ("zp", c, max(tt - 1, 0)), ("zph", c)], writes=[("ps", b1)])
                    P.act(lambda e, b1=b1, c=c, tt=tt: e.activation(out=abuf[:, 2 + c, tt * 512:(tt + 1) * 512], in_=psum[b1][:, :], func=AF.Identity, bias=0.0, scale=vec256p[:, l, 0, c:c + 1]),
                          reads=[("ps", b1), "vec256p"], writes=[("AB", 2 + c, tt), ("ps", b1)])
                    if g == 0 and tt == 0:
                        b2 = bank()
                        for j in range(nsh):
                            P.pe(lambda e, c=c, j=j, b2=b2, base=base, nsh=nsh: e.matmul(psum[b2][:, 0:16], lhsT=(poolP0[:, c, :] if j == 0 else poolL[:, base + j, :]), rhs=zpbuf[:, c, 16 - j:32 - j], start=(j == 0), stop=(j == nsh - 1)),
                                 reads=[("poolP", c), ("poolL", c), ("zp", c, 0), ("zph", c)], writes=[("ps", b2)])
                        P.dve(lambda e, c=c, b2=b2: e.tensor_tensor(out=c16[:, c, :], in0=psum[b2][:, 0:16], in1=pcm1[:, c, :], op=ALU.mult), reads=[("ps", b2), "pcm1"], writes=[("c16", c)])
                        P.dve(lambda e, c=c, b1=b1: e.tensor_tensor(out=c16b[:, c, :], in0=psum[b1][:, 0:16], in1=c16[:, c, :], op=ALU.add), reads=[("ps", b1), ("c16", c)], writes=[("c16b", c)])
                        P.dve(lambda e, c=c: e.tensor_scalar_mul(out=abuf[:, 2 + c, 0:16], in0=c16b[:, c, :], scalar1=vec256p[:, l, 0, c:c + 1]),
                              reads=[("c16b", c), "vec256p", ("AB", 2 + c, 0)], writes=[("AB", 2 + c, 0)])
            chk("m_pool")
            def ret_tile(n):
                tt, nn = n // 4, n % 4
                bA, bB = bank(), bank()
                si = 0
                for hd in range(4):
                    c, h = hd // 2, hd % 2
                    hs = slice(h * 64, (h + 1) * 64)
                    bh = bA if h == 0 else bB
                    P.pe(lambda e, c=c, hs=hs, n=n, bh=bh: e.matmul(psum[bh][:, c * 128:(c + 1) * 128], lhsT=abuf[hs, KR + c, n * 128:(n + 1) * 128], rhs=abuf[hs, QR + c, n * 128:(n + 1) * 128], start=True, stop=True),
                         reads=[("AB", KR + c, tt), ("AB", QR + c, tt)], writes=[("ps", bh)])
                yield
                for h in range(2):
                    bh = bA if h == 0 else bB
                    P.dve(lambda e, h=h, bh=bh, si=si: e.tensor_tensor(out=smb[:, si, :].rearrange("p (c h m) -> p c h m", c=2, h=2)[:, :, h, :],
                                                                     in0=psum[bh][:, 0:256].rearrange("p (c m) -> p c m", c=2),
                                                                     in1=decT[:, :, :].rearrange("p (c h) m -> p c h m", h=2)[:, :, h, :], op=ALU.mult),
                          reads=[("ps", bh), "decT"], writes=[("smb", si, h)])
                yield
                b2 = bank()
                for hd in range(4):
                    c, h = hd // 2, hd % 2
                    hs = slice(h * 64, (h + 1) * 64)
                    osl = psum[b2][:, hd * 64:(hd + 1) * 64]
                    P.pe(lambda e, hd=hd, n=n, si=si, osl=osl: e.matmul(osl, lhsT=smb[:, si, hd * 128:(hd + 1) * 128], rhs=vtk[:, n, hd * 64:(hd + 1) * 64], start=True, stop=False),
                         reads=[("smb", si, 0), ("smb", si, 1), ("AB", VTK, n)], writes=[("ps", b2)])
                    P.pe(lambda e, c=c, hs=hs, n=n, osl=osl: e.matmul(osl, lhsT=abuf[hs, QX + c, n * 128:(n + 1) * 128], rhs=stb[hs, n, c, :], start=False, stop=False),
                         reads=[("AB", QX + c, tt), ("stb", n)], writes=[("ps", b2)])
                    P.pe(lambda e, c=c, hs=hs, n=n, osl=osl: e.matmul(osl, lhsT=abuf[hs, QX + c, n * 128:(n + 1) * 128], rhs=sinb[hs, c, n, :], start=False, stop=True),
                         reads=[("AB", QX + c, tt), "sinb"], writes=[("ps", b2)])
                yield
                o3 = psum[b2][:, 0:256].rearrange("p (h e) -> p h e", h=4)
                P.dve(lambda e, o3=o3: e.reduce_sum(out=gst[:, 8:12], in_=o3, axis=AX.X), reads=[("ps", b2)], writes=["gst2"])
                P.act(lambda e, b2=b2: e.activation(out=gsq[:], in_=psum[b2][:, 0:256], func=AF.Square), reads=[("ps", b2), "gsq"], writes=["gsq", ("ps", b2)])
                yield
                P.dve(lambda e: e.reduce_sum(out=gst[:, 12:16], in_=gsq[:].rearrange("p (h e) -> p h e", h=4), axis=AX.X), reads=["gsq", "gst2"], writes=["gst2"])
                P.dve(lambda e: e.tensor_scalar_mul(out=gst[:, 8:16], in0=gst[:, 8:16], scalar1=1.0 / 64.0), reads=["gst2"], writes=["gst2"])
                P.dve(lambda e: e.tensor_tensor(out=gst[:, 0:4], in0=gst[:, 8:12], in1=gst[:, 8:12], op=ALU.mult), reads=["gst2", "gst"], writes=["gst"])
                P.dve(lambda e: e.tensor_tensor(out=gst[:, 0:4], in0=gst[:, 12:16], in1=gst[:, 0:4], op=ALU.subtract), reads=["gst2", "gst"], writes=["gst"])
                yield
                P.act(lambda e: e.activation(out=gst[:, 4:8], in_=gst[:, 0:4], func=AF.Sqrt, bias=GN_EPS, scale=1.0), reads=["gst"], writes=["gst"])
                yield
                P.dve(lambda e: e.reciprocal(out=gst[:, 4:8], in_=gst[:, 4:8]), reads=["gst"], writes=["gst"])
                g3 = gtmp[:].rearrange("p (h e) -> p h e", h=4)
                P.dve(lambda e, o3=o3, g3=g3: e.tensor_tensor(out=g3, in0=o3, in1=gst[:, 8:12].unsqueeze(2).to_broadcast([128, 4, 64]), op=ALU.subtract),
                      reads=[("ps", b2), "gst2", "gtmp"], writes=["gtmp"])
                P.dve(lambda e, g3=g3: e.tensor_tensor(out=g3, in0=g3, in1=gst[:, 4:8].unsqueeze(2).to_broadcast([128, 4, 64]), op=ALU.mult), reads=["gtmp", "gst"], writes=["gtmp"])
                P.dve(lambda e: e.tensor_tensor(out=gtmp[:], in0=gtmp[:], in1=gnb[:, 0, :], op=ALU.mult), reads=["gtmp", ("gnb", 0)], writes=["gtmp"])
                P.dve(lambda e: e.tensor_tensor(out=gtmp[:], in0=gtmp[:], in1=gnb[:, 1, :], op=ALU.add), reads=["gtmp", ("gnb", 1)], writes=["gtmp"])
                P.dve(lambda e, n=n, si=si: e.tensor_tensor(out=yrt[:, si, :], in0=sgt[:, n, :], in1=gtmp[:], op=ALU.mult), reads=["gtmp", ("AB", SGT, n)], writes=[("yrt", si)])
                yield
                b3, tv = tbank()
                for c in range(2):
                    P.pe(lambda e, c=c, si=si, tv=tv: e.transpose(tv[:, c * 128:(c + 1) * 128], yrt[:, si, c * 128:(c + 1) * 128], ident[:]), reads=[("yrt", si), "ident"], writes=[("ps", b3)])
                yield
                P.act(lambda e, n=n, tv=tv: e.copy(out=abuf[:, 4:6, n * 128:(n + 1) * 128], in_=tv[:, 0:256].rearrange("p (c t) -> p c t", c=2)), reads=[("ps", b3)],
                      writes=[("AB", 4, tt, nn), ("AB", 5, tt, nn)])
                yield

            alive = [chain([sgu_tile(n) for n in range(n_solo, 8)]), chain([ret_tile(n) for n in range(8)])]
            while alive:
                for gg in list(alive):
                    try:
                        next(gg)
                    except StopIteration:
                        alive.remove(gg)
            chk("m_ret")
            wbrv = wbr_d[l].rearrange("n (c p) d -> p n c d", p=128)
            wgv = wg_d[l].rearrange("(k p) (n d) -> p k n d", p=128, n=4)
            wov = wo_d[l].rearrange("(k p) c -> p k c", p=128)
            ykeys = [("AB", j, tt) for j in range(8) for tt in range(2)] + [("AB", j, tt, nn) for j in (4, 5, 6, 7) for tt in range(2) for nn in range(4)]
            for dp in range(4):
                sb_, kb_ = wload([(lambda r: r[:, 0:2048].rearrange("p (n c d) -> p n c d", n=4, c=2), wbrv[:, :, :, dp * 256:(dp + 1) * 256])])
                rvb = ring[sb_][:, 0:2048].rearrange("p (n c d) -> p n c d", n=4, c=2)
                for dd in range(2):
                    d = 2 * dp + dd
                    s_, kg = wload([(lambda r, nq=nq: r[:, 0:4096].rearrange("p (n k d) -> p n k d", n=4, k=8)[:, nq, :, :],
                                     wgv[:, :, nq, d * 128:(d + 1) * 128]) for nq in range(4)])
                    rvg = ring[s_][:, 0:4096].rearrange("p (n k d) -> p n k d", n=4, k=8)
                    for tt in range(2):
                        for nb in range(4):
                            bgt, bbr = bank(), bank()
                            mm_group(psum[bgt][:, :], [(rvg[:, nb, k, :], xb[:, k, tt * 512:(tt + 1) * 512]) for k in range(8)], kg + xbk, bgt)
                            mm_group(psum[bbr][:, :], [(rvb[:, nb, c, dd * 128:(dd + 1) * 128], abuf[:, 2 * nb + c, tt * 512:(tt + 1) * 512]) for c in range(2)], kb_ + ykeys, bbr)
                            gi = nscr()
                            P.act(lambda e, gi=gi, bgt=bgt, nb=nb, d=d: e.activation(out=scr[gi][:], in_=psum[bgt][:, :], func=AF.Sigmoid, bias=bgate[:, l, nb * 8 + d:nb * 8 + d + 1], scale=1.0),
                                  reads=[("ps", bgt), "bgate"], writes=[("scr", gi)])
                            if nb == 0:
                                P.dve(lambda e, gi=gi, bbr=bbr: e.tensor_tensor(out=macc[:], in0=psum[bbr][:, :], in1=scr[gi][:], op=ALU.mult), reads=[("ps", bbr), ("scr", gi)], writes=["macc"])
                            else:
                                si = gi
                                P.dve(lambda e, gi=gi, bbr=bbr, si=si: e.tensor_tensor(out=scr[si][:], in0=psum[bbr][:, :], in1=scr[gi][:], op=ALU.mult), reads=[("ps", bbr), ("scr", gi)], writes=[("scr", si)])
                                if nb < 3:
                                    P.dve(lambda e, si=si: e.tensor_tensor(out=macc[:], in0=macc[:], in1=scr[si][:], op=ALU.add), reads=["macc", ("scr", si)], writes=["macc"])
                                else:
                                    P.dve(lambda e, si=si, d=d, tt=tt: e.tensor_tensor(out=mTg[tt][:, d, :], in0=macc[:], in1=scr[si][:], op=ALU.add),
                                          reads=["macc", ("scr", si)], writes=[("mT", tt, d)] + mt_alias[tt])
            for dp in range(4):
                s_, k_ = wload([(lambda r: r[:, 0:2048].rearrange("p (k d) -> p k d", k=8), wov[:, :, dp * 256:(dp + 1) * 256])])
                rvo = ring[s_][:, 0:2048].rearrange("p (k d) -> p k d", k=8)
                for dd in range(2):
                    d = 2 * dp + dd
                    for tt in range(2):
                        tt4 = 2 * g + tt
                        b = bank()
                        mm_group(psum[b][:, :], [(rvo[:, k, dd * 128:(dd + 1) * 128], mTg[tt][:, k, :]) for k in range(8)], k_ + [("mT", tt, k) for k in range(8)] + mt_alias[tt], b)
                        P.dve(lambda e, b=b, d=d, tt4=tt4: e.scalar_tensor_tensor(out=x[:, d, tks(tt4)], in0=psum[b][:, :], scalar=1.0 / ALPHA, in1=x[:, d, tks(tt4)], op0=ALU.mult, op1=ALU.add),
                              reads=[("ps", b), ("x", d, tt4)], writes=[("x", d, tt4)])
            stage_end_ln(l, 1, g)

        mTg = [abuf[:, 8 + 4 * t_:12 + 4 * t_, :].rearrange("p a (h t) -> p (a h) t", t=512) for t_ in range(2)]
        mt_alias = [[("AB", j, t_) for j in (8, 9, 10, 11) for t_ in range(2)],
                    [("AB", j, t_) for j in (12, 13) for t_ in range(2)] + [("AB", KTZ, n_) for n_ in range(8)]]

        def program():
          for l in range(depth):
            mixer_setup(l)
            for g in range(NG):
                ffn(l, 0, g)
            if stop == "ffn1":
                break
            mixer_pre(l)
            if stop == "pre":
                break
            mixer(l, 0)
            if stop == "mix0":
                break
            mixer(l, 1)
            if stop == "mix":
                break
            for g in range(NG):
                ffn(l, 1, g)

        try:
            program()
        except StopBuild:
            pass
        drain()

        for k in range(8):
            P.dma("sp", "out", out_d.rearrange("(k p) t -> p k t", p=128)[:, k, :], x[:, k, :], reads=[("x", k, t) for t in range(4)], writes=[("out", k)])
        P.add("sp", None, reads=[("out", k) for k in range(8)])
        P.emit(nc, st)
    return nc


def host_tables(half):
    f32 = np.float32
    p = np.arange(128)
    inv = (10000.0 ** (-(np.arange(32, dtype=f32)) / f32(32))).astype(f32)
    pos = (half * T + np.arange(T)).astype(f32)
    ang = (pos[None, :] * inv[p % 32][:, None]).astype(f32)
    cos = np.cos(ang).astype(f32)
    sin = np.sin(ang).astype(f32)
    sgn = np.where((p % 64) < 32, -1.0, 1.0).astype(f32)
    rope = np.stack([cos, sin * sgn[:, None]], axis=1).astype(f32)
    gam = (1.0 - 2.0 ** (-5.0 - np.arange(4))).astype(np.float64)
    c = np.arange(128)
    decT = np.zeros((128, 4, 128), np.float64)
    for h in range(4):
        dm = c[None, :] - c[:, None]
        decT[:, h, :] = np.where(dm >= 0, gam[h] ** np.maximum(dm, 0), 0.0) / 8.0
    xi = np.zeros((128, 2, 128), np.float64)
    cdec = np.zeros((128, 4), np.float64)
    sinsc = np.zeros((128, 2, 16), np.float64)
    for ch in range(2):
        for hh in range(2):
            h = 2 * ch + hh
            xi[hh * 64:(hh + 1) * 64, ch, :] = gam[h] ** (c + 1.0)
            cdec[hh * 64:(hh + 1) * 64, ch] = gam[h] ** 128.0
            cdec[hh * 64:(hh + 1) * 64, 2 + ch] = gam[h] ** 1024.0
            sinsc[hh * 64:(hh + 1) * 64, ch, :] = (gam[h] ** (128.0 * np.arange(16))) * float(half)
    zeta = np.zeros((128, 256), np.float64)
    for h in range(4):
        zeta[:, h * 64:(h + 1) * 64] = (gam[h] ** (127.0 - c))[:, None] / 8.0
    pcm1 = np.zeros((128, 2, 16), np.float64)
    if half == 0:
        t = np.arange(16)
        for ch in range(2):
            for hh in range(2):
                w = POOL_W[2 * ch + hh]
                pcm1[hh * 64:(hh + 1) * 64, ch, :] = w / np.minimum(t + 1, w) - 1.0
    flag = np.full((128, 1), float(half))
    return {"rope": rope, "decT": decT.astype(f32), "xi": xi.astype(f32), "zeta8": zeta.astype(f32), "cdec": cdec.astype(f32),
            "sinsc": sinsc.astype(f32), "pcm1": pcm1.astype(f32), "flag": flag.astype(f32)}


def host_layout(inputs, depth=DEPTH):
    f32 = np.float32
    g = lambda k: np.asarray(inputs[k], dtype=f32)[:depth]
    w_in = g("w_in")
    qs, ks = 1024, 1280
    perm = np.concatenate([h * 64 + (np.arange(64) + 32) % 64 for h in range(4)])
    w_in_ext = np.ascontiguousarray(np.concatenate([w_in, w_in[:, :, qs + perm], w_in[:, :, ks + perm]], axis=2))
    def pp(a):
        sh = a.shape
        a = a.reshape(sh[:-1] + (sh[-1] // 128, 128))
        return np.ascontiguousarray(np.moveaxis(a, -1, 0))
    vec256 = np.ascontiguousarray(np.stack([g("pool_scale"), g("ret_gn_g"), g("ret_gn_b"), g("sgu_ln_g"), g("sgu_ln_b")], axis=1))
    shared = {
        "ffn1_w1": g("ffn1_w1"), "ffn2_w1": g("ffn2_w1"), "ffn1_w2": g("ffn1_w2"), "ffn2_w2": g("ffn2_w2"),
        "w_in_ext": w_in_ext, "w_gate": g("w_gate"), "w_branch": g("w_branch"), "w_out": g("w_out"),
        "lnp": pp(np.stack([g("ln1_g"), g("ln1_b"), g("ln2_g"), g("ln2_b"), g("ln3_g"), g("ln3_b")], axis=1)),
        "b_gate": pp(g("b_gate")), "conv_w": pp(g("conv_w")), "pool_w": g("pool_w"),
        "vec256": vec256, "vec256p": pp(vec256),
        "sgu_wT": np.ascontiguousarray(np.transpose(g("sgu_w"), (0, 1, 3, 2))),
        "sgu_b": g("sgu_b"),
    }
    return shared


_NC_CACHE = {}
SAFE_STOP = None


def kernel(**inputs):
    x = np.asarray(inputs["x"], dtype=np.float32)
    shared = host_layout(inputs)
    tabs = [host_tables(0), host_tables(1)]
    in_maps = []
    for c in range(8):
        b, half = c // 2, c % 2
        m = dict(shared)
        m.update(tabs[half])
        m["xT"] = np.ascontiguousarray(x[b, half * T:(half + 1) * T, :].T)
        in_maps.append(m)
    if "nc" not in _NC_CACHE:
        _NC_CACHE["nc"] = build(stop=SAFE_STOP)
    res = run_bass_kernel_spmd(_NC_CACHE["nc"], in_maps, core_ids=list(range(8)))
    out = np.empty_like(x)
    for c in range(8):
        b, half = c // 2, c % 2
        out[b, half * T:(half + 1) * T, :] = res.results[c]["outT"].T
    return out
```

```python
import numpy as np
from contextlib import ExitStack
import concourse.bass as bass
import concourse.mybir as mybir
from concourse.bass_utils import run_bass_kernel_spmd

F32 = mybir.dt.float32
BF16 = mybir.dt.bfloat16
AF = mybir.ActivationFunctionType
ALU = mybir.AluOpType
AX = mybir.AxisListType

D = 1024
T = 2048
NG = 2
TG = 1024
DEPTH = 4
FF = 2816
NJ = 22
ALPHA = (2.0 * DEPTH) ** 0.25
LN_EPS = 1e-5
GN_EPS = 1e-5
POOL_W = (2, 4, 8, 16)
XW = 168
NSLOT = 3
SLOT = 4096


class StopBuild(Exception):
    pass


class Op:
    __slots__ = ("eng", "fn", "deps", "signal", "count", "dsem", "inc")

    def __init__(self, eng, fn, dsem=None, inc=16):
        self.eng = eng
        self.fn = fn
        self.deps = set()
        self.signal = False
        self.count = 0
        self.dsem = dsem
        self.inc = inc


class Prog:
    ENGS = ("pe", "act", "dve", "pool", "sp")

    def __init__(self):
        self.ops = []
        self.res = {}
        self.dma_keys = []

    def add(self, eng, fn, reads=(), writes=(), dsem=None, inc=16):
        op = Op(eng, fn, dsem, inc)
        if dsem is not None and dsem not in self.dma_keys:
            self.dma_keys.append(dsem)
        deps = set()
        for k in reads:
            st = self.res.get(k)
            if st is not None and st[0] is not None:
                deps.add(st[0])
        for k in writes:
            st = self.res.get(k)
            if st is not None:
                if st[0] is not None:
                    deps.add(st[0])
                for r in st[1]:
                    deps.add(r)
        for d in deps:
            if d is op:
                continue
            if d.dsem is None and d.eng == eng and eng == "pe":
                continue
            op.deps.add(d)
        for k in reads:
            st = self.res.setdefault(k, [None, []])
            st[1].append(op)
        for k in writes:
            self.res[k] = [op, []]
        self.ops.append(op)
        return op

    def pe(self, fn, reads=(), writes=()):
        return self.add("pe", fn, reads, writes)

    def act(self, fn, reads=(), writes=()):
        return self.add("act", fn, reads, writes)

    def dve(self, fn, reads=(), writes=()):
        return self.add("dve", fn, reads, writes)

    def dma(self, queue, dsem, out, in_, reads=(), writes=()):
        return self.add(queue, lambda e: e.dma_start(out=out, in_=in_), reads, writes, dsem=dsem)

    def emit(self, nc, stack):
        for op in self.ops:
            for d in op.deps:
                d.signal = True
        eng_cnt = {e: 0 for e in self.ENGS}
        dma_cnt = {k: 0 for k in self.dma_keys}
        needs = {}
        for op in self.ops:
            nd = {}
            for d in op.deps:
                if d.dsem is not None:
                    nd[("d", d.dsem)] = dma_cnt[d.dsem]
            needs[id(op)] = nd
            if op.dsem is not None:
                dma_cnt[op.dsem] += op.inc
                op.count = dma_cnt[op.dsem]
            elif op.signal:
                eng_cnt[op.eng] += 1
                op.count = eng_cnt[op.eng]
        sems = {}
        for e in self.ENGS:
            sems[e] = stack.enter_context(nc.semaphore("s_" + e))
        for i, k in enumerate(self.dma_keys):
            sems[("d", k)] = stack.enter_context(nc.semaphore("d%d" % i))
        per_eng = {e: [] for e in self.ENGS}
        for op in self.ops:
            per_eng[op.eng].append(op)

        def run(engname, e):
            known = {}
            for op in per_eng[engname]:
                need = dict(needs[id(op)])
                for d in op.deps:
                    if d.dsem is not None:
                        continue
                    if d.count > need.get(d.eng, 0):
                        need[d.eng] = d.count
                for sk, v in need.items():
                    if known.get(sk, 0) >= v:
                        continue
                    e.wait_ge(sems[sk], v)
                    known[sk] = v
                if op.fn is None:
                    continue
                ins = op.fn(e)
                if op.dsem is not None:
                    ins.then_inc(sems[("d", op.dsem)], op.inc)
                elif op.signal:
                    ins.then_inc(sems[op.eng], 1)

        block = stack.enter_context(nc.Block())

        @block.tensor
        def _(e):
            run("pe", e)

        @block.scalar
        def _(e):
            run("act", e)

        @block.vector
        def _(e):
            run("dve", e)

        @block.gpsimd
        def _(e):
            run("pool", e)

        @block.sync
        def _(e):
            run("sp", e)


def build(depth=DEPTH, dbg=None, stop=None):
    nc = bass.Bass("TRN2", target_bir_lowering=False)
    P = Prog()

    def din(name, shape):
        return nc.dram_tensor(name, list(shape), F32, kind="ExternalInput").ap()

    xT_d = din("xT", [D, T])
    w1_d = [din("ffn1_w1", [depth, D, 2 * FF]), din("ffn2_w1", [depth, D, 2 * FF])]
    w2_d = [din("ffn1_w2", [depth, FF, D]), din("ffn2_w2", [depth, FF, D])]
    win_d = din("w_in_ext", [depth, D, 3072])
    wg_d = din("w_gate", [depth, D, 4 * D])
    wbr_d = din("w_branch", [depth, 4, 256, D])
    wo_d = din("w_out", [depth, D, D])
    lnp_d = din("lnp", [128, depth, 6, 8])
    bg_d = din("b_gate", [128, depth, 32])
    convw_d = din("conv_w", [128, depth, 3, 2])
    poolw_d = din("pool_w", [depth, 4, 64, 64])
    vec256_d = din("vec256", [depth, 5, 256])
    vec256p_d = din("vec256p", [128, depth, 5, 2])
    sguwT_d = din("sgu_wT", [depth, 4, 128, 128])
    sgub_d = din("sgu_b", [depth, 4, 128])
    rope_d = din("rope", [128, 2, T])
    decT_d = din("decT", [128, 4, 128])
    xi_d = din("xi", [128, 2, 128])
    zeta_d = din("zeta8", [128, 256])
    cdec_d = din("cdec", [128, 4])
    sinsc_d = din("sinsc", [128, 2, 16])
    pcm1_d = din("pcm1", [128, 2, 16])
    flag_d = din("flag", [128, 1])
    out_d = nc.dram_tensor("outT", [D, T], F32, kind="ExternalOutput").ap()
    if dbg is not None:
        dbg_d = nc.dram_tensor("dbg", [128, dbg], F32, kind="ExternalOutput").ap()

    st = ExitStack()
    with st:
        def sb(name, shape, dt=F32):
            return st.enter_context(nc.sbuf_tensor("s_" + name, list(shape), dt))

        x = sb("x", [128, 8, T])
        xb = sb("xb", [128, 8, TG], BF16)
        abuf = sb("abuf", [128, NJ, TG], BF16)
        ring = [sb("ring%d" % s, [128, SLOT], BF16) for s in range(NSLOT)]
        pbuf = sb("pbuf", [128, 2, 16 + TG], BF16)
        zpbuf = sb("zpbuf", [128, 2, 16 + TG], BF16)
        vnb = sb("vnb", [128, 2, 256], BF16)
        rope = sb("rope", [128, 2, 512])
        lnrb = sb("lnrb", [128, 1, 512], BF16)
        lnsq = sb("lnsq", [128, 1, 512], BF16)
        scr = [sb("scr%d" % i, [128, 512]) for i in range(4)]
        lnm = sb("lnm", [128, 512]); lnr = sb("lnr", [128, 512]); lnt = lnr
        ident = sb("ident", [128, 128], BF16); identf = scr[1][:, 0:128]; poolf = scr[0][:, 0:256].rearrange("p (c d) -> p c d", c=2)
        onesD = sb("onesD", [128, 128], BF16)
        lnp = sb("lnp", [128, depth, 6, 8])
        bgate = sb("bgate", [128, depth, 32])
        convw = sb("convw", [128, depth, 3, 2])
        vec256p = sb("vec256p", [128, depth, 5, 2])
        diag = sb("diag", [128, 2, 3, 128], BF16)
        poolL = sb("poolL", [128, 20, 128], BF16)
        poolP0 = sb("poolP0", [128, 2, 128], BF16)
        sguw = sb("sguw", [128, 4, 128], BF16)
        sgubT = sb("sgubT", [128, 2, 128])
        gnb = sb("gnb", [128, 4, 256])
        decT = sb("decT", [128, 4, 128]); xi = sb("xi", [128, 2, 128]); zeta = sb("zeta", [128, 256])
        cdec = sb("cdec", [128, 4]); sinsc = sb("sinsc", [128, 2, 16]); pcm1 = sb("pcm1", [128, 2, 16])
        flag = sb("flag", [128, 1])
        stf = sb("stf", [128, 2, 64])
        stb = sb("stb", [128, 8, 2, 64], BF16)
        sinb = sb("sinb", [128, 2, 8, 64], BF16)
        snd = sb("snd", [128, XW]); rcv = sb("rcv", [128, XW])
        gstS = sb("gstS", [128, 8]); gst = sb("gst", [128, 16]); gsq = sb("gsq", [128, 256]); gtmp = sb("gtmp", [128, 256]); gtmp2 = gsq
        smb = sb("smb", [128, 1, 512], BF16)
        yrt = sb("yrt", [128, 1, 256], BF16)
        c16 = sb("c16", [128, 2, 16]); c16b = sb("c16b", [128, 2, 16])
        macc = sb("macc", [128, 512])

        def aview(j0, nj):
            return abuf[:, j0:j0 + nj, :]
        yall = abuf
        QR, QX, KR = 8, 10, 12
        KTZ, VTK, SGT, VNT = 14, 16, 18, 20

        def tokview(j0):
            return abuf[:, j0:j0 + 2, :].rearrange("p a (n c) -> p (a n) c", c=256)

        psum = [st.enter_context(nc.psum_tensor("ps%d" % i, [128, 512], F32)) for i in range(6)]
        psln = [st.enter_context(nc.psum_tensor("psln%d" % i, [128, 512], F32)) for i in range(2)]
        bank_ctr = [0]

        def bank():
            b = bank_ctr[0] % 6
            bank_ctr[0] += 1
            return b

        slot_ctr = [0]

        def wload(parts):
            s = slot_ctr[0] % NSLOT
            slot_ctr[0] += 1
            keys = []
            for i, (vf, src) in enumerate(parts):
                k = ("ws", s, i)
                keys.append(k)
                P.dma("pool", ("w", s), vf(ring[s]), src, writes=[k])
            return s, keys

        scr_ctr = [0]

        def nscr():
            i = scr_ctr[0] % 4
            scr_ctr[0] += 1
            return i

        P.dma("sp", "c0", x[:, :, :], xT_d.rearrange("(k p) t -> p k t", p=128), writes=[("x", k, t) for k in range(8) for t in range(4)])
        P.dma("sp", "c1", lnp[:, :, :, :], lnp_d[:, :, :, :], writes=["lnp"])
        P.dma("sp", "c1", bgate[:, :, :], bg_d[:, :, :], writes=["bgate"])
        P.dma("sp", "c1", convw[:, :, :, :], convw_d[:, :, :, :], writes=["convw"])
        P.dma("sp", "c1", vec256p[:, :, :, :], vec256p_d[:, :, :, :], writes=["vec256p"])
        P.dma("sp", "c1", decT[:, :, :], decT_d[:, :, :], writes=["decT"])
        P.dma("sp", "c1", xi[:, :, :], xi_d[:, :, :], writes=["xi"])
        P.dma("sp", "c1", zeta[:, :], zeta_d[:, :], writes=["zeta"])
        P.dma("sp", "c1", cdec[:, :], cdec_d[:, :], writes=["cdec"])
        P.dma("sp", "c1", sinsc[:, :, :], sinsc_d[:, :, :], writes=["sinsc"])
        P.dma("sp", "c1", pcm1[:, :, :], pcm1_d[:, :, :], writes=["pcm1"])
        P.dma("sp", "c1", flag[:, :], flag_d[:, :], writes=["flag"])
        P.add("pool", lambda e: e.memset(identf, 0.0), writes=[("scr", 1)])
        P.add("pool", lambda e: e.affine_select(out=identf, in_=identf, pattern=[[-1, 128]], compare_op=ALU.not_equal,
                                                fill=1.0, base=0, channel_multiplier=1), reads=[("scr", 1)], writes=[("scr", 1)])
        P.dve(lambda e: e.tensor_copy(out=ident[:], in_=identf), reads=[("scr", 1)], writes=["ident"])
        P.dve(lambda e: e.memset(onesD[:], 1.0 / 1024.0), writes=["onesD"])

        def tks(tt4):
            return slice(tt4 * 512, (tt4 + 1) * 512)

        def cast_xb(g):
            for k in range(8):
                P.act(lambda e, k=k: e.copy(out=xb[:, k, :], in_=x[:, k, g * TG:(g + 1) * TG]),
                      reads=[("x", k, 2 * g), ("x", k, 2 * g + 1)], writes=[("xb", k)])

        def mm_group(out_ap, pairs, reads, b):
            n = len(pairs)
            for i, (l_ap, r_ap) in enumerate(pairs):
                P.pe(lambda e, l_ap=l_ap, r_ap=r_ap, i=i: e.matmul(out_ap, lhsT=l_ap, rhs=r_ap, start=(i == 0), stop=(i == n - 1)),
                     reads=reads, writes=[("ps", b)])
            tick(1)

        pending = []
        pend_group = [None]

        def tick(n=1):
            for _ in range(n):
                if not pending:
                    return
                try:
                    next(pending[0])
                except StopIteration:
                    pending.pop(0)

        def drain():
            while pending:
                tick()
            pend_group[0] = None

        def stage_begin(g):
            if pending and pend_group[0] == g:
                drain()

        def stage_end_ln(l, which, g):
            drain()
            for tt in range(2):
                pending.append(layer_norm(l, which, g, tt))
            pend_group[0] = g

        def layer_norm(l, which, g, tt):
            tt4 = 2 * g + tt
            ts = tks(tt4)
            eps = LN_EPS / (ALPHA * ALPHA)
            for k in range(8):
                i = 0
                P.act(lambda e, k=k, i=i: e.copy(out=lnrb[:, i, :], in_=x[:, k, ts]), reads=[("x", k, tt4)], writes=[("lnrb", i)])
                P.act(lambda e, k=k, i=i: e.activation(out=lnsq[:, i, :], in_=x[:, k, ts], func=AF.Square), reads=[("x", k, tt4)], writes=[("lnsq", i)])
                yield
                P.pe(lambda e, k=k, i=i: e.matmul(psln[0][:, :], lhsT=onesD[:, :], rhs=lnrb[:, i, :], start=(k == 0), stop=(k == 7)),
                     reads=[("lnrb", i), "onesD"], writes=[("psln", 0)])
                P.pe(lambda e, k=k, i=i: e.matmul(psln[1][:, :], lhsT=onesD[:, :], rhs=lnsq[:, i, :], start=(k == 0), stop=(k == 7)),
                     reads=[("lnsq", i), "onesD"], writes=[("psln", 1)])
                yield
            P.act(lambda e: e.copy(out=lnm[:], in_=psln[0][:, :]), reads=[("psln", 0)], writes=["lnm"])
            P.dve(lambda e: e.tensor_tensor(out=lnr[:], in0=lnm[:], in1=lnm[:], op=ALU.mult), reads=["lnm"], writes=["lnr"])
            P.dve(lambda e: e.tensor_tensor(out=lnr[:], in0=psln[1][:, :], in1=lnr[:], op=ALU.subtract), reads=[("psln", 1), "lnr"], writes=["lnr"])
            yield
            P.act(lambda e: e.activation(out=lnr[:], in_=lnr[:], func=AF.Sqrt, bias=eps, scale=1.0), reads=["lnr"], writes=["lnr"])
            P.dve(lambda e: e.reciprocal(out=lnr[:], in_=lnr[:]), reads=["lnr"], writes=["lnr"])
            yield
            for k in range(8):
                si = nscr()
                P.dve(lambda e, k=k, si=si: e.tensor_tensor(out=scr[si][:], in0=x[:, k, ts], in1=lnm[:], op=ALU.subtract),
                      reads=[("x", k, tt4), "lnm"], writes=[("scr", si)])
                P.dve(lambda e, si=si: e.tensor_tensor(out=scr[si][:], in0=scr[si][:], in1=lnr[:], op=ALU.mult),
                      reads=[("scr", si), "lnr"], writes=[("scr", si)])
                P.act(lambda e, k=k, si=si: e.activation(out=x[:, k, ts], in_=scr[si][:], func=AF.Identity,
                                                         bias=lnp[:, l, 2 * which + 1, k:k + 1], scale=lnp[:, l, 2 * which, k:k + 1]),
                      reads=[("scr", si), "lnp"], writes=[("x", k, tt4)])
                yield

        def ffn(l, f, g):
            stage_begin(g)
            cast_xb(g)
            w1v = w1_d[f][l].rearrange("(k p) c -> p k c", p=128)
            w2v = w2_d[f][l].rearrange("(j p) d -> p j d", p=128)
            xbk = [("xb", k) for k in range(8)]
            for jb in range(NJ // 2):
                j0 = 2 * jb
                s, keys = wload([
                    (lambda r: r[:, :].rearrange("p (k q c) -> p k q c", k=8, q=2)[:, :, 0, :], w1v[:, :, j0 * 128:(j0 + 2) * 128]),
                    (lambda r: r[:, :].rearrange("p (k q c) -> p k q c", k=8, q=2)[:, :, 1, :], w1v[:, :, FF + j0 * 128:FF + (j0 + 2) * 128]),
                ])
                rv = ring[s][:, :].rearrange("p (k q c) -> p k q c", k=8, q=2)
                for jj in range(2):
                    j = j0 + jj
                    for tt in range(2):
                        bg_, bu_ = bank(), bank()
                        mm_group(psum[bg_][:, :], [(rv[:, k, 0, jj * 128:(jj + 1) * 128], xb[:, k, tt * 512:(tt + 1) * 512]) for k in range(8)], keys + xbk, bg_)
                        mm_group(psum[bu_][:, :], [(rv[:, k, 1, jj * 128:(jj + 1) * 128], xb[:, k, tt * 512:(tt + 1) * 512]) for k in range(8)], keys + xbk, bu_)
                        si = nscr()
                        P.act(lambda e, si=si, bg_=bg_: e.activation(out=scr[si][:], in_=psum[bg_][:, :], func=AF.Silu), reads=[("ps", bg_)], writes=[("scr", si)])
                        P.dve(lambda e, si=si, bu_=bu_, j=j, tt=tt: e.tensor_tensor(out=abuf[:, j, tt * 512:(tt + 1) * 512], in0=psum[bu_][:, :], in1=scr[si][:], op=ALU.mult),
                              reads=[("ps", bu_), ("scr", si)], writes=[("AB", j, tt)])
            for d in range(8):
                s_, keys = wload([(lambda r, hh=hh: r[:, 0:NJ * 128].rearrange("p (j c) -> p j c", j=NJ)[:, hh * 11:(hh + 1) * 11, :],
                                   w2v[:, hh * 11:(hh + 1) * 11, d * 128:(d + 1) * 128]) for hh in range(2)])
                rv = ring[s_][:, 0:NJ * 128].rearrange("p (j c) -> p j c", j=NJ)
                for tt in range(2):
                    tt4 = 2 * g + tt
                    b = bank()
                    pairs = [(rv[:, j, :], abuf[:, j, tt * 512:(tt + 1) * 512]) for j in range(NJ)]
                    mm_group(psum[b][:, :], pairs, keys + [("AB", j, tt) for j in range(NJ)], b)
                    P.dve(lambda e, b=b, d=d, tt4=tt4: e.scalar_tensor_tensor(out=x[:, d, tks(tt4)], in0=psum[b][:, :], scalar=0.5 / ALPHA, in1=x[:, d, tks(tt4)],
                                                                          op0=ALU.mult, op1=ALU.add),
                          reads=[("ps", b), ("x", d, tt4)], writes=[("x", d, tt4)])
            stage_end_ln(l, 2 * f, g)

        stf2 = sb("stf2", [128, 2, 64])

        def mixer_setup(l):
            for c in range(2):
                for j in range(3):
                    P.dve(lambda e, c=c, j=j: e.tensor_scalar_mul(out=diag[:, c, j, :], in0=ident[:], scalar1=convw[:, l, j, c:c + 1]),
                          reads=["ident", "convw"], writes=[("diag", c)])
            P.dve(lambda e: e.memset(poolf, 0.0), writes=[("scr", 0)])
            for gq in range(4):
                c, h = gq // 2, gq % 2
                P.dma("sp", "c2p", poolf[h * 64:(h + 1) * 64, c, h * 64:(h + 1) * 64], poolw_d[l, gq], reads=[("scr", 0)], writes=[("poolf", gq)])
            pk = [("poolf", q) for q in range(4)] + [("scr", 0)]
            for c in range(2):
                base = 0 if c == 0 else 4
                nsh = 4 if c == 0 else 16
                for j in range(nsh):
                    for h in range(2):
                        w = POOL_W[2 * c + h]
                        sc = (1.0 / w if j < w else 0.0)
                        hs = slice(h * 64, (h + 1) * 64)
                        if j == 0:
                            P.dve(lambda e, c=c, hs=hs, sc=sc: e.tensor_scalar_mul(out=poolP0[hs, c, :], in0=poolf[hs, c, :], scalar1=sc),
                                  reads=pk, writes=[("poolP", c)])
                        P.dve(lambda e, c=c, j=j, hs=hs, sc=sc, base=base: e.tensor_scalar_mul(out=poolL[hs, base + j, :], in0=poolf[hs, c, :], scalar1=(sc - 1.0 if j == 0 else sc)),
                              reads=pk, writes=[("poolL", c)])
            P.dma("pool", "c3", sguw[:, :, :], sguwT_d[l].rearrange("q j i -> j q i"), writes=["sguw"])
            P.dve(lambda e: e.memset(sguw[64:128, :, 0:64], 0.0), reads=["sguw"], writes=["sguw"])
            for c in range(2):
                for h in range(2):
                    P.dma("sp", "c2s", sgubT[h * 64:(h + 1) * 64, c, :], sgub_d[l, 2 * c + h:2 * c + h + 1, :].partition_broadcast(64), writes=[("sgubT", c, h)])
            for i in range(4):
                P.dma("sp", "c2g", gnb[:, i, :], vec256_d[l, i + 1:i + 2, :].partition_broadcast(128), writes=[("gnb", i)])

        def tbank():
            b = bank()
            return b, psum[b][:, :].bitcast(BF16)

        class MX:
            pass

        def mx_ctx(l):
            m = MX()
            m.xbk = [("xb", k) for k in range(8)]
            winv = win_d[l].rearrange("(k p) c -> p k c", p=128)

            def wcols(c0, ncol):
                s, keys = wload([(lambda r: r[:, 0:8 * ncol].rearrange("p (k c) -> p k c", k=8), winv[:, :, c0:c0 + ncol])])
                return ring[s][:, 0:8 * ncol].rearrange("p (k c) -> p k c", k=8), keys

            def proj_fm(rv, keys, ci, tt, b):
                mm_group(psum[b][:, :], [(rv[:, k, ci * 128:(ci + 1) * 128], xb[:, k, tt * 512:(tt + 1) * 512]) for k in range(8)], keys + m.xbk, b)
            m.wcols, m.proj_fm = wcols, proj_fm
            return m

        def rot_qk(m, g, tts, do_q=True):
            rv_qk, k_qk = m.wcols(1024, 512)
            rv_pp, k_pp = m.wcols(2560, 512)
            for tt in tts:
                tt4 = 2 * g + tt
                P.dma("sp", "rope", rope[:, :, :], rope_d[:, :, tt4 * 512:(tt4 + 1) * 512], writes=["rope"])
                for qk in ((0, 1) if do_q else (1,)):
                    for c in range(2):
                        b1, b2 = bank(), bank()
                        m.proj_fm(rv_qk, k_qk, 2 * qk + c, tt, b1)
                        m.proj_fm(rv_pp, k_pp, 2 * qk + c, tt, b2)
                        s1, s2 = nscr(), nscr()
                        P.dve(lambda e, s1=s1, b1=b1: e.tensor_tensor(out=scr[s1][:], in0=psum[b1][:, :], in1=rope[:, 0, :], op=ALU.mult),
                              reads=[("ps", b1), "rope"], writes=[("scr", s1)])
                        P.dve(lambda e, s2=s2, b2=b2: e.tensor_tensor(out=scr[s2][:], in0=psum[b2][:, :], in1=rope[:, 1, :], op=ALU.mult),
                              reads=[("ps", b2), "rope"], writes=[("scr", s2)])
                        dst = QR if qk == 0 else KR
                        P.dve(lambda e, s1=s1, s2=s2, dst=dst, c=c, tt=tt: e.tensor_tensor(out=abuf[:, dst + c, tt * 512:(tt + 1) * 512], in0=scr[s1][:], in1=scr[s2][:], op=ALU.add),
                              reads=[("scr", s1), ("scr", s2)], writes=[("AB", dst + c, tt)])
                        if qk == 0:
                            P.dve(lambda e, c=c, tt=tt: e.tensor_tensor(
                                out=abuf[:, QX + c, tt * 512:(tt + 1) * 512].rearrange("p (n c) -> p n c", c=128),
                                in0=abuf[:, QR + c, tt * 512:(tt + 1) * 512].rearrange("p (n c) -> p n c", c=128),
                                in1=xi[:, c, :].unsqueeze(1).to_broadcast([128, 4, 128]), op=ALU.mult),
                                reads=[("AB", QR + c, tt), "xi"], writes=[("AB", QX + c, tt)])

        def k_transposes():
            ktz = tokview(KTZ)
            for n in range(8):
                tt = n // 4
                b, tv = tbank()
                for c in range(2):
                    P.pe(lambda e, c=c, n=n, tv=tv: e.transpose(tv[:, c * 128:(c + 1) * 128], abuf[:, KR + c, n * 128:(n + 1) * 128], ident[:]),
                         reads=[("AB", KR + c, tt), "ident"], writes=[("ps", b)])
                P.dve(lambda e, n=n, tv=tv: e.tensor_tensor(out=ktz[:, n, :], in0=tv[:, 0:256], in1=zeta[:], op=ALU.mult), reads=[("ps", b), "zeta"], writes=[("AB", KTZ, n)])

        def kv_scan(stt, skey, store_base=None):
            ktz, vtk = tokview(KTZ), tokview(VTK)
            for n in range(8):
                b1 = bank()
                for c in range(2):
                    P.pe(lambda e, c=c, n=n, b1=b1: e.matmul(psum[b1][:, c * 128:(c + 1) * 128], lhsT=ktz[:, n, c * 128:(c + 1) * 128], rhs=vtk[:, n, c * 128:(c + 1) * 128], start=True, stop=True),
                         reads=[("AB", KTZ, n), ("AB", VTK, n)], writes=[("ps", b1)])
                if store_base is not None:
                    ng = store_base + n
                    P.act(lambda e, ng=ng: e.copy(out=stb[:, ng, :, :], in_=stt[:, :, :]), reads=[skey], writes=[("stb", ng)])
                for c in range(2):
                    for h in range(2):
                        hs = slice(h * 64, (h + 1) * 64)
                        P.dve(lambda e, c=c, h=h, hs=hs, b1=b1: e.scalar_tensor_tensor(out=stt[hs, c, :], in0=stt[hs, c, :], scalar=cdec[hs, c:c + 1],
                                                                                     in1=psum[b1][hs, c * 128 + h * 64:c * 128 + (h + 1) * 64], op0=ALU.mult, op1=ALU.add),
                              reads=[skey, ("ps", b1), "cdec"], writes=[skey])

        def mixer_pre(l):
            g = 1
            stage_begin(g)
            cast_xb(g)
            m = mx_ctx(l)
            P.dve(lambda e: e.memset(snd[:], 0.0), writes=["snd"])
            rv_bg, k_bg = m.wcols(0, 512)
            rv_x, k_x = m.wcols(512, 512)
            for c in range(2):
                b1, b2, b3 = bank(), bank(), bank()
                m.proj_fm(rv_bg, k_bg, 2 + c, 1, b1)
                m.proj_fm(rv_x, k_x, c, 1, b2)
                m.proj_fm(rv_x, k_x, 2 + c, 1, b3)
                si = nscr()
                P.act(lambda e, si=si, b1=b1: e.copy(out=scr[si][:, 0:2], in_=psum[b1][:, 510:512]), reads=[("ps", b1)], writes=[("scr", si)])
                P.dve(lambda e, si=si, b2=b2, c=c: e.tensor_tensor(out=snd[:, 160 + 2 * c:162 + 2 * c], in0=psum[b2][:, 510:512], in1=scr[si][:, 0:2], op=ALU.mult),
                      reads=[("ps", b2), ("scr", si), "snd"], writes=[("snd", 1 + c)])
                P.act(lambda e, b3=b3, c=c: e.copy(out=snd[:, 128 + 16 * c:144 + 16 * c], in_=psum[b3][:, 496:512]), reads=[("ps", b3), "snd"], writes=[("snd", 3 + c)])
            rot_qk(m, g, (0, 1), do_q=False)
            rv_v, k_v = m.wcols(1536, 256)
            vtk = tokview(VTK)
            for n in range(8):
                b1 = bank()
                mm_group(psum[b1][:, 0:256], [(xb[:, k, n * 128:(n + 1) * 128], rv_v[:, k, :]) for k in range(8)], k_v + m.xbk, b1)
                P.act(lambda e, b1=b1, n=n: e.copy(out=vtk[:, n, :], in_=psum[b1][:, 0:256]), reads=[("ps", b1)], writes=[("AB", VTK, n)])
            k_transposes()
            P.dve(lambda e: e.memset(stf2[:], 0.0), writes=["stf2"])
            kv_scan(stf2, "stf2")

        def chk(name):
            if stop == name:
                raise StopBuild()

        def mixer(l, g):
            stage_begin(g)
            cast_xb(g)
            m = mx_ctx(l)
            xbk = m.xbk
            ktz, vtk, sgt = tokview(KTZ), tokview(VTK), tokview(SGT)
            uT = abuf[:, 20:22, :]
            if g == 1:
                for c in range(2):
                    P.dve(lambda e, c=c: e.tensor_copy(out=zpbuf[:, c, 0:16], in_=zpbuf[:, c, TG:16 + TG]),
                          reads=[("zp", c, 1)], writes=[("zph", c)])
                    P.dve(lambda e, c=c: e.tensor_copy(out=pbuf[:, c, 14:16], in_=pbuf[:, c, 16 + TG - 2:16 + TG]),
                          reads=[("pb", c, 1)], writes=[("ph", c)])
            rv_bg, k_bg = m.wcols(0, 512)
            rv_x, k_x = m.wcols(512, 512)
            for c in range(2):
                for tt in range(2):
                    b1, b2 = bank(), bank()
                    m.proj_fm(rv_bg, k_bg, 2 + c, tt, b1)
                    m.proj_fm(rv_x, k_x, c, tt, b2)
                    si = nscr()
                    P.act(lambda e, si=si, b1=b1: e.copy(out=scr[si][:], in_=psum[b1][:, :]), reads=[("ps", b1)], writes=[("scr", si)])
                    P.dve(lambda e, si=si, b2=b2, c=c, tt=tt: e.tensor_tensor(out=pbuf[:, c, 16 + tt * 512:16 + (tt + 1) * 512], in0=psum[b2][:, :], in1=scr[si][:], op=ALU.mult),
                          reads=[("ps", b2), ("scr", si)], writes=[("pb", c, tt)])
            for c in range(2):
                for tt in range(2):
                    b1 = bank()
                    m.proj_fm(rv_x, k_x, 2 + c, tt, b1)
                    P.act(lambda e, b1=b1, c=c, tt=tt: e.copy(out=zpbuf[:, c, 16 + tt * 512:16 + (tt + 1) * 512], in_=psum[b1][:, :]),
                          reads=[("ps", b1)], writes=[("zp", c, tt)])
            rot_qk(m, g, (0, 1), do_q=True)
            rv_vg, k_vg = m.wcols(1536, 512)
            for n in range(8):
                b1 = bank()
                mm_group(psum[b1][:, :], [(xb[:, k, n * 128:(n + 1) * 128], rv_vg[:, k, :]) for k in range(8)], k_vg + xbk, b1)
                P.act(lambda e, b1=b1, n=n: e.copy(out=vtk[:, n, :], in_=psum[b1][:, 0:256]), reads=[("ps", b1)], writes=[("AB", VTK, n)])
                P.act(lambda e, b1=b1, n=n: e.activation(out=sgt[:, n, :], in_=psum[b1][:, 256:512], func=AF.Silu), reads=[("ps", b1)], writes=[("AB", SGT, n)])
            k_transposes()
            if g == 0:
                P.dve(lambda e: e.memset(stf[:], 0.0), writes=["stf"])
            kv_scan(stf, "stf", store_base=0)
            chk("m_front")
            if g == 0:
                for c in range(2):
                    P.dve(lambda e, c=c: e.scalar_tensor_tensor(out=snd[:, 64 * c:64 * (c + 1)], in0=stf[:, c, :], scalar=cdec[:, 2 + c:3 + c], in1=stf2[:, c, :], op0=ALU.mult, op1=ALU.add),
                          reads=["stf", "stf2", "cdec", "snd"], writes=[("snd", 5 + c)])
                xs = nc.dram_tensor("xsnd%d" % l, [128, XW], F32)
                xr = nc.dram_tensor("xrcv%d" % l, [256, XW], F32)
                P.dma("sp", "xs", xs.ap()[:, :], snd[:], reads=["snd"] + [("snd", i) for i in range(1, 7)], writes=[("xsd", l)])
                P.add("pool", lambda e: e.collective_compute("AllGather", ALU.bypass, replica_groups=[[0, 1], [2, 3], [4, 5], [6, 7]],
                                                             ins=[xs.ap().opt()], outs=[xr.ap().opt()]),
                      reads=[("xsd", l)], writes=[("xrd", l)], dsem=("cc", l), inc=1)
                P.dma("sp", "xr", rcv[:], xr.ap()[0:128, :], reads=[("xrd", l)], writes=["rcv"])
            chk("m_xch")
            rv_uv, k_uv = m.wcols(2048, 512)
            for c in range(2):
                for tt in range(2):
                    b1 = bank()
                    m.proj_fm(rv_uv, k_uv, c, tt, b1)
                    P.act(lambda e, b1=b1, c=c, tt=tt: e.activation(out=uT[:, c, tt * 512:(tt + 1) * 512], in_=psum[b1][:, :], func=AF.Gelu_apprx_tanh),
                          reads=[("ps", b1)], writes=[("uT", c, tt)])
            chk("s_u")
            drain()
            gsqS = lnrb[:, 0, :].bitcast(F32)
            gtmpS = lnsq[:, 0, :].bitcast(F32)
            KQ, KT = ("lnrb", 0), ("lnsq", 0)

            def sgu_tile(n):
                b1 = bank()
                mm_group(psum[b1][:, 0:256], [(xb[:, k, n * 128:(n + 1) * 128], rv_uv[:, k, 256:512]) for k in range(8)], k_uv + xbk, b1)
                yield
                P.act(lambda e, b1=b1: e.activation(out=gtmpS, in_=psum[b1][:, 0:256], func=AF.Gelu_apprx_tanh), reads=[("ps", b1)], writes=[KT])
                yield
                P.dve(lambda e: e.reduce_sum(out=gstS[:, 0:1], in_=gtmpS, axis=AX.X), reads=[KT], writes=["gstS"])
                P.act(lambda e: e.activation(out=gsqS, in_=gtmpS, func=AF.Square), reads=[KT], writes=[KQ])
                yield
                P.dve(lambda e: e.reduce_sum(out=gstS[:, 1:2], in_=gsqS, axis=AX.X), reads=[KQ, "gstS"], writes=["gstS"])
                P.dve(lambda e: e.tensor_scalar_mul(out=gstS[:, 2:4], in0=gstS[:, 0:2], scalar1=1.0 / 256.0), reads=["gstS"], writes=["gstS"])
                P.dve(lambda e: e.tensor_tensor(out=gstS[:, 4:5], in0=gstS[:, 2:3], in1=gstS[:, 2:3], op=ALU.mult), reads=["gstS"], writes=["gstS"])
                P.dve(lambda e: e.tensor_tensor(out=gstS[:, 5:6], in0=gstS[:, 3:4], in1=gstS[:, 4:5], op=ALU.subtract), reads=["gstS"], writes=["gstS"])
                yield
                P.act(lambda e: e.activation(out=gstS[:, 6:7], in_=gstS[:, 5:6], func=AF.Sqrt, bias=LN_EPS, scale=1.0), reads=["gstS"], writes=["gstS"])
                yield
                P.dve(lambda e: e.reciprocal(out=gstS[:, 7:8], in_=gstS[:, 6:7]), reads=["gstS"], writes=["gstS"])
                P.dve(lambda e: e.tensor_scalar(out=gsqS, in0=gtmpS, scalar1=gstS[:, 2:3], scalar2=gstS[:, 7:8], op0=ALU.subtract, op1=ALU.mult),
                      reads=[KT, "gstS", KQ], writes=[KQ])
                P.dve(lambda e: e.tensor_tensor(out=gsqS, in0=gsqS, in1=gnb[:, 2, :], op=ALU.mult), reads=[KQ, ("gnb", 2)], writes=[KQ])
                P.dve(lambda e, n=n: e.tensor_tensor(out=vnb[:, n % 2, :], in0=gsqS, in1=gnb[:, 3, :], op=ALU.add), reads=[KQ, ("gnb", 3)], writes=[("vnb", n % 2)])
                yield
                b2 = bank()
                for c in range(2):
                    for h in range(2):
                        P.pe(lambda e, c=c, h=h, n=n, b2=b2: e.matmul(psum[b2][:, (2 * c + h) * 128:(2 * c + h + 1) * 128], lhsT=vnb[:, n % 2, c * 128:(c + 1) * 128], rhs=sguw[:, 2 * c + h, :], start=True, stop=True),
                             reads=[("vnb", n % 2), "sguw"], writes=[("ps", b2)])
                yield
                tt, nn = n // 4, n % 4
                for c in range(2):
                    for h in range(2):
                        hs = slice(h * 64, (h + 1) * 64)
                        P.dve(lambda e, c=c, h=h, hs=hs, b2=b2: e.tensor_tensor(out=gtmpS[hs, c * 128:(c + 1) * 128], in0=psum[b2][hs, (2 * c + h) * 128:(2 * c + h + 1) * 128], in1=sgubT[hs, c, :], op=ALU.add),
                              reads=[("ps", b2), ("sgubT", c, h), KT], writes=[KT])
                    P.dve(lambda e, c=c, n=n: e.tensor_tensor(out=abuf[:, 6 + c, n * 128:(n + 1) * 128], in0=uT[:, c, n * 128:(n + 1) * 128], in1=gtmpS[:, c * 128:(c + 1) * 128], op=ALU.mult),
                          reads=[KT, ("uT", c, tt)], writes=[("AB", 6 + c, tt, nn)])
                yield

            def chain(gens):
                for gg in gens:
                    yield from gg

            n_solo = 4 if g == 0 else 0
            for _ in chain([sgu_tile(n) for n in range(n_solo)]):
                pass
            chk("m_sgu")
            if g == 0:
                for c in range(2):
                    P.dve(lambda e, c=c: e.tensor_scalar_mul(out=zpbuf[:, c, 0:16], in0=rcv[:, 128 + 16 * c:128 + 16 * (c + 1)], scalar1=flag[:, 0:1]),
                          reads=["rcv", "flag"], writes=[("zph", c)])
                    P.dve(lambda e, c=c: e.tensor_scalar_mul(out=pbuf[:, c, 14:16], in0=rcv[:, 160 + 2 * c:160 + 2 * (c + 1)], scalar1=flag[:, 0:1]),
                          reads=["rcv", "flag"], writes=[("ph", c)])

            P.dve(lambda e: e.tensor_tensor(out=sinb[:, :, :, :],
                                            in0=rcv[:, 0:128].rearrange("p (c e) -> p c e", c=2).unsqueeze(2).to_broadcast([128, 2, 8, 64]),
                                            in1=sinsc[:, :, 8 * g:8 * g + 8].unsqueeze(3).to_broadcast([128, 2, 8, 64]), op=ALU.mult),
                  reads=["rcv", "sinsc"], writes=["sinb"])
            rv_b, k_b = m.wcols(0, 256)
            for c in range(2):
                for tt in range(2):
                    b1, b2 = bank(), bank()
                    for j in range(3):
                        P.pe(lambda e, c=c, tt=tt, j=j, b1=b1: e.matmul(psum[b1][:, :], lhsT=diag[:, c, j, :], rhs=pbuf[:, c, 14 + j + tt * 512:14 + j + (tt + 1) * 512], start=(j == 0), stop=(j == 2)),
                             reads=[("diag", c), ("pb", c, tt), ("pb", c, max(tt - 1, 0)), ("ph", c)], writes=[("ps", b1)])
                    m.proj_fm(rv_b, k_b, c, tt, b2)
                    si = nscr()
                    P.act(lambda e, si=si, b2=b2: e.copy(out=scr[si][:], in_=psum[b2][:, :]), reads=[("ps", b2)], writes=[("scr", si)])
                    P.dve(lambda e, si=si, b1=b1, c=c, tt=tt: e.tensor_tensor(out=abuf[:, 0 + c, tt * 512:(tt + 1) * 512], in0=psum[b1][:, :], in1=scr[si][:], op=ALU.mult),
                          reads=[("ps", b1), ("scr", si)], writes=[("AB", 0 + c, tt)])
            chk("m_conv")
            for c in range(2):
                base = 0 if c == 0 else 4
                nsh = 4 if c == 0 else 16
                for tt in range(2):
                    b1 = bank()
                    for j in range(nsh):
                        P.pe(lambda e, c=c, tt=tt, j=j, b1=b1, base=base, nsh=nsh: e.matmul(psum[b1][:, :], lhsT=poolL[:, base + j, :], rhs=zpbuf[:, c, 16 - j + tt * 512:16 - j + (tt + 1) * 512], start=(j == 0), stop=(j == nsh - 1)),
                             reads=[("poolL", c), ("zp", c, tt), ("zp", c, max(tt - 1, 0)), ("zph", c)], writes=[("ps", b1)])
                    P.act(lambda e, b1=b1, c=c, tt=tt: e.activation(out=abuf[:, 2 + c, tt * 512:(tt + 1) * 512], in_=psum[b1][:, :], func=AF.Identity, bias=0.0, scale=vec256p[:, l, 0, c:c + 1]),
                          reads=[("ps", b1), "vec256p"], writes=[("AB", 2 + c, tt), ("ps", b1)])
                    if g == 0 and tt == 0:
                        b2 = bank()
                        for j in range(nsh):
                            P.pe(lambda e, c=c, j=j, b2=b2, base=base, nsh=nsh: e.matmul(psum[b2][:, 0:16], lhsT=(poolP0[:, c, :] if j == 0 else poolL[:, base + j, :]), rhs=zpbuf[:, c, 16 - j:32 - j], start=(j == 0), stop=(j == nsh - 1)),
                                 reads=[("poolP", c), ("poolL", c), ("zp", c, 0), ("zph", c)], writes=[("ps", b2)])
                        P.dve(lambda e, c=c, b2=b2: e.tensor_tensor(out=c16[:, c, :], in0=psum[b2][:, 0:16], in1=pcm1[:, c, :], op=ALU.mult), reads=[("ps", b2), "pcm1"], writes=[("c16", c)])
                        P.dve(lambda e, c=c, b1=b1: e.tensor_tensor(out=c16b[:, c, :], in0=psum[b1][:, 0:16], in1=c16[:, c, :], op=ALU.add), reads=[("ps", b1), ("c16", c)], writes=[("c16b", c)])
                        P.dve(lambda e, c=c: e.tensor_scalar_mul(out=abuf[:, 2 + c, 0:16], in0=c16b[:, c, :], scalar1=vec256p[:, l, 0, c:c + 1]),
                              reads=[("c16b", c), "vec256p", ("AB", 2 + c, 0)], writes=[("AB", 2 + c, 0)])
            chk("m_pool")
            def ret_tile(n):
                tt, nn = n // 4, n % 4
                bA, bB = bank(), bank()
                si = 0
                for hd in range(4):
                    c, h = hd // 2, hd % 2
                    hs = slice(h * 64, (h + 1) * 64)
                    bh = bA if h == 0 else bB
                    P.pe(lambda e, c=c, hs=hs, n=n, bh=bh: e.matmul(psum[bh][:, c * 128:(c + 1) * 128], lhsT=abuf[hs, KR + c, n * 128:(n + 1) * 128], rhs=abuf[hs, QR + c, n * 128:(n + 1) * 128], start=True, stop=True),
                         reads=[("AB", KR + c, tt), ("AB", QR + c, tt)], writes=[("ps", bh)])
                yield
                for h in range(2):
                    bh = bA if h == 0 else bB
                    P.dve(lambda e, h=h, bh=bh, si=si: e.tensor_tensor(out=smb[:, si, :].rearrange("p (c h m) -> p c h m", c=2, h=2)[:, :, h, :],
                                                                     in0=psum[bh][:, 0:256].rearrange("p (c m) -> p c m", c=2),
                                                                     in1=decT[:, :, :].rearrange("p (c h) m -> p c h m", h=2)[:, :, h, :], op=ALU.mult),
                          reads=[("ps", bh), "decT"], writes=[("smb", si, h)])
                yield
                b2 = bank()
                for hd in range(4):
                    c, h = hd // 2, hd % 2
                    hs = slice(h * 64, (h + 1) * 64)
                    osl = psum[b2][:, hd * 64:(hd + 1) * 64]
                    P.pe(lambda e, hd=hd, n=n, si=si, osl=osl: e.matmul(osl, lhsT=smb[:, si, hd * 128:(hd + 1) * 128], rhs=vtk[:, n, hd * 64:(hd + 1) * 64], start=True, stop=False),
                         reads=[("smb", si, 0), ("smb", si, 1), ("AB", VTK, n)], writes=[("ps", b2)])
                    P.pe(lambda e, c=c, hs=hs, n=n, osl=osl: e.matmul(osl, lhsT=abuf[hs, QX + c, n * 128:(n + 1) * 128], rhs=stb[hs, n, c, :], start=False, stop=False),
                         reads=[("AB", QX + c, tt), ("stb", n)], writes=[("ps", b2)])
                    P.pe(lambda e, c=c, hs=hs, n=n, osl=osl: e.matmul(osl, lhsT=abuf[hs, QX + c, n * 128:(n + 1) * 128], rhs=sinb[hs, c, n, :], start=False, stop=True),
                         reads=[("AB", QX + c, tt), "sinb"], writes=[("ps", b2)])
                yield
                o3 = psum[b2][:, 0:256].rearrange("p (h e) -> p h e", h=4)
                P.dve(lambda e, o3=o3: e.reduce_sum(out=gst[:, 8:12], in_=o3, axis=AX.X), reads=[("ps", b2)], writes=["gst2"])
                P.act(lambda e, b2=b2: e.activation(out=gsq[:], in_=psum[b2][:, 0:256], func=AF.Square), reads=[("ps", b2), "gsq"], writes=["gsq", ("ps", b2)])
                yield
                P.dve(lambda e: e.reduce_sum(out=gst[:, 12:16], in_=gsq[:].rearrange("p (h e) -> p h e", h=4), axis=AX.X), reads=["gsq", "gst2"], writes=["gst2"])
                P.dve(lambda e: e.tensor_scalar_mul(out=gst[:, 8:16], in0=gst[:, 8:16], scalar1=1.0 / 64.0), reads=["gst2"], writes=["gst2"])
                P.dve(lambda e: e.tensor_tensor(out=gst[:, 0:4], in0=gst[:, 8:12], in1=gst[:, 8:12], op=ALU.mult), reads=["gst2", "gst"], writes=["gst"])
                P.dve(lambda e: e.tensor_tensor(out=gst[:, 0:4], in0=gst[:, 12:16], in1=gst[:, 0:4], op=ALU.subtract), reads=["gst2", "gst"], writes=["gst"])
                yield
                P.act(lambda e: e.activation(out=gst[:, 4:8], in_=gst[:, 0:4], func=AF.Sqrt, bias=GN_EPS, scale=1.0), reads=["gst"], writes=["gst"])
                yield
                P.dve(lambda e: e.reciprocal(out=gst[:, 4:8], in_=gst[:, 4:8]), reads=["gst"], writes=["gst"])
                g3 = gtmp[:].rearrange("p (h e) -> p h e", h=4)
                P.dve(lambda e, o3=o3, g3=g3: e.tensor_tensor(out=g3, in0=o3, in1=gst[:, 8:12].unsqueeze(2).to_broadcast([128, 4, 64]), op=ALU.subtract),
                      reads=[("ps", b2), "gst2", "gtmp"], writes=["gtmp"])
                P.dve(lambda e, g3=g3: e.tensor_tensor(out=g3, in0=g3, in1=gst[:, 4:8].unsqueeze(2).to_broadcast([128, 4, 64]), op=ALU.mult), reads=["gtmp", "gst"], writes=["gtmp"])
                P.dve(lambda e: e.tensor_tensor(out=gtmp[:], in0=gtmp[:], in1=gnb[:, 0, :], op=ALU.mult), reads=["gtmp", ("gnb", 0)], writes=["gtmp"])
                P.dve(lambda e: e.tensor_tensor(out=gtmp[:], in0=gtmp[:], in1=gnb[:, 1, :], op=ALU.add), reads=["gtmp", ("gnb", 1)], writes=["gtmp"])
                P.dve(lambda e, n=n, si=si: e.tensor_tensor(out=yrt[:, si, :], in0=sgt[:, n, :], in1=gtmp[:], op=ALU.mult), reads=["gtmp", ("AB", SGT, n)], writes=[("yrt", si)])
                yield
                b3, tv = tbank()
                for c in range(2):
                    P.pe(lambda e, c=c, si=si, tv=tv: e.transpose(tv[:, c * 128:(c + 1) * 128], yrt[:, si, c * 128:(c + 1) * 128], ident[:]), reads=[("yrt", si), "ident"], writes=[("ps", b3)])
                yield
                P.act(lambda e, n=n, tv=tv: e.copy(out=abuf[:, 4:6, n * 128:(n + 1) * 128], in_=tv[:, 0:256].rearrange("p (c t) -> p c t", c=2)), reads=[("ps", b3)],
                      writes=[("AB", 4, tt, nn), ("AB", 5, tt, nn)])
                yield

            alive = [chain([sgu_tile(n) for n in range(n_solo, 8)]), chain([ret_tile(n) for n in range(8)])]
            while alive:
                for gg in list(alive):
                    try:
                        next(gg)
                    except StopIteration:
                        alive.remove(gg)
            chk("m_ret")
            wbrv = wbr_d[l].rearrange("n (c p) d -> p n c d", p=128)
            wgv = wg_d[l].rearrange("(k p) (n d) -> p k n d", p=128, n=4)
            wov = wo_d[l].rearrange("(k p) c -> p k c", p=128)
            ykeys = [("AB", j, tt) for j in range(8) for tt in range(2)] + [("AB", j, tt, nn) for j in (4, 5, 6, 7) for tt in range(2) for nn in range(4)]
            for dp in range(4):
                sb_, kb_ = wload([(lambda r: r[:, 0:2048].rearrange("p (n c d) -> p n c d", n=4, c=2), wbrv[:, :, :, dp * 256:(dp + 1) * 256])])
                rvb = ring[sb_][:, 0:2048].rearrange("p (n c d) -> p n c d", n=4, c=2)
                for dd in range(2):
                    d = 2 * dp + dd
                    s_, kg = wload([(lambda r, nq=nq: r[:, 0:4096].rearrange("p (n k d) -> p n k d", n=4, k=8)[:, nq, :, :],
                                     wgv[:, :, nq, d * 128:(d + 1) * 128]) for nq in range(4)])
                    rvg = ring[s_][:, 0:4096].rearrange("p (n k d) -> p n k d", n=4, k=8)
                    for tt in range(2):
                        for nb in range(4):
                            bgt, bbr = bank(), bank()
                            mm_group(psum[bgt][:, :], [(rvg[:, nb, k, :], xb[:, k, tt * 512:(tt + 1) * 512]) for k in range(8)], kg + xbk, bgt)
                            mm_group(psum[bbr][:, :], [(rvb[:, nb, c, dd * 128:(dd + 1) * 128], abuf[:, 2 * nb + c, tt * 512:(tt + 1) * 512]) for c in range(2)], kb_ + ykeys, bbr)
                            gi = nscr()
                            P.act(lambda e, gi=gi, bgt=bgt, nb=nb, d=d: e.activation(out=scr[gi][:], in_=psum[bgt][:, :], func=AF.Sigmoid, bias=bgate[:, l, nb * 8 + d:nb * 8 + d + 1], scale=1.0),
                                  reads=[("ps", bgt), "bgate"], writes=[("scr", gi)])
                            if nb == 0:
                                P.dve(lambda e, gi=gi, bbr=bbr: e.tensor_tensor(out=macc[:], in0=psum[bbr][:, :], in1=scr[gi][:], op=ALU.mult), reads=[("ps", bbr), ("scr", gi)], writes=["macc"])
                            else:
                                si = gi
                                P.dve(lambda e, gi=gi, bbr=bbr, si=si: e.tensor_tensor(out=scr[si][:], in0=psum[bbr][:, :], in1=scr[gi][:], op=ALU.mult), reads=[("ps", bbr), ("scr", gi)], writes=[("scr", si)])
                                if nb < 3:
                                    P.dve(lambda e, si=si: e.tensor_tensor(out=macc[:], in0=macc[:], in1=scr[si][:], op=ALU.add), reads=["macc", ("scr", si)], writes=["macc"])
                                else:
                                    P.dve(lambda e, si=si, d=d, tt=tt: e.tensor_tensor(out=mTg[tt][:, d, :], in0=macc[:], in1=scr[si][:], op=ALU.add),
                                          reads=["macc", ("scr", si)], writes=[("mT", tt, d)] + mt_alias[tt])
            for dp in range(4):
                s_, k_ = wload([(lambda r: r[:, 0:2048].rearrange("p (k d) -> p k d", k=8), wov[:, :, dp * 256:(dp + 1) * 256])])
                rvo = ring[s_][:, 0:2048].rearrange("p (k d) -> p k d", k=8)
                for dd in range(2):
                    d = 2 * dp + dd
                    for tt in range(2):
                        tt4 = 2 * g + tt
                        b = bank()
                        mm_group(psum[b][:, :], [(rvo[:, k, dd * 128:(dd + 1) * 128], mTg[tt][:, k, :]) for k in range(8)], k_ + [("mT", tt, k) for k in range(8)] + mt_alias[tt], b)
                        P.dve(lambda e, b=b, d=d, tt4=tt4: e.scalar_tensor_tensor(out=x[:, d, tks(tt4)], in0=psum[b][:, :], scalar=1.0 / ALPHA, in1=x[:, d, tks(tt4)], op0=ALU.mult, op1=ALU.add),
                              reads=[("ps", b), ("x", d, tt4)], writes=[("x", d, tt4)])
            stage_end_ln(l, 1, g)

        mTg = [abuf[:, 8 + 4 * t_:12 + 4 * t_, :].rearrange("p a (h t) -> p (a h) t", t=512) for t_ in range(2)]
        mt_alias = [[("AB", j, t_) for j in (8, 9, 10, 11) for t_ in range(2)],
                    [("AB", j, t_) for j in (12, 13) for t_ in range(2)] + [("AB", KTZ, n_) for n_ in range(8)]]

        def program():
          for l in range(depth):
            mixer_setup(l)
            for g in range(NG):
                ffn(l, 0, g)
            if stop == "ffn1":
                break
            mixer_pre(l)
            if stop == "pre":
                break
            mixer(l, 0)
            if stop == "mix0":
                break
            mixer(l, 1)
            if stop == "mix":
                break
            for g in range(NG):
                ffn(l, 1, g)

        try:
            program()
        except StopBuild:
            pass
        drain()

        for k in range(8):
            P.dma("sp", "out", out_d.rearrange("(k p) t -> p k t", p=128)[:, k, :], x[:, k, :], reads=[("x", k, t) for t in range(4)], writes=[("out", k)])
        P.add("sp", None, reads=[("out", k) for k in range(8)])
        P.emit(nc, st)
    return nc


def host_tables(half):
    f32 = np.float32
    p = np.arange(128)
    inv = (10000.0 ** (-(np.arange(32, dtype=f32)) / f32(32))).astype(f32)
    pos = (half * T + np.arange(T)).astype(f32)
    ang = (pos[None, :] * inv[p % 32][:, None]).astype(f32)
    cos = np.cos(ang).astype(f32)
    sin = np.sin(ang).astype(f32)
    sgn = np.where((p % 64) < 32, -1.0, 1.0).astype(f32)
    rope = np.stack([cos, sin * sgn[:, None]], axis=1).astype(f32)
    gam = (1.0 - 2.0 ** (-5.0 - np.arange(4))).astype(np.float64)
    c = np.arange(128)
    decT = np.zeros((128, 4, 128), np.float64)
    for h in range(4):
        dm = c[None, :] - c[:, None]
        decT[:, h, :] = np.where(dm >= 0, gam[h] ** np.maximum(dm, 0), 0.0) / 8.0
    xi = np.zeros((128, 2, 128), np.float64)
    cdec = np.zeros((128, 4), np.float64)
    sinsc = np.zeros((128, 2, 16), np.float64)
    for ch in range(2):
        for hh in range(2):
            h = 2 * ch + hh
            xi[hh * 64:(hh + 1) * 64, ch, :] = gam[h] ** (c + 1.0)
            cdec[hh * 64:(hh + 1) * 64, ch] = gam[h] ** 128.0
            cdec[hh * 64:(hh + 1) * 64, 2 + ch] = gam[h] ** 1024.0
            sinsc[hh * 64:(hh + 1) * 64, ch, :] = (gam[h] ** (128.0 * np.arange(16))) * float(half)
    zeta = np.zeros((128, 256), np.float64)
    for h in range(4):
        zeta[:, h * 64:(h + 1) * 64] = (gam[h] ** (127.0 - c))[:, None] / 8.0
    pcm1 = np.zeros((128, 2, 16), np.float64)
    if half == 0:
        t = np.arange(16)
        for ch in range(2):
            for hh in range(2):
                w = POOL_W[2 * ch + hh]
                pcm1[hh * 64:(hh + 1) * 64, ch, :] = w / np.minimum(t + 1, w) - 1.0
    flag = np.full((128, 1), float(half))
    return {"rope": rope, "decT": decT.astype(f32), "xi": xi.astype(f32), "zeta8": zeta.astype(f32), "cdec": cdec.astype(f32),
            "sinsc": sinsc.astype(f32), "pcm1": pcm1.astype(f32), "flag": flag.astype(f32)}


def host_layout(inputs, depth=DEPTH):
    f32 = np.float32
    g = lambda k: np.asarray(inputs[k], dtype=f32)[:depth]
    w_in = g("w_in")
    qs, ks = 1024, 1280
    perm = np.concatenate([h * 64 + (np.arange(64) + 32) % 64 for h in range(4)])
    w_in_ext = np.ascontiguousarray(np.concatenate([w_in, w_in[:, :, qs + perm], w_in[:, :, ks + perm]], axis=2))
    def pp(a):
        sh = a.shape
        a = a.reshape(sh[:-1] + (sh[-1] // 128, 128))
        return np.ascontiguousarray(np.moveaxis(a, -1, 0))
    vec256 = np.ascontiguousarray(np.stack([g("pool_scale"), g("ret_gn_g"), g("ret_gn_b"), g("sgu_ln_g"), g("sgu_ln_b")], axis=1))
    shared = {
        "ffn1_w1": g("ffn1_w1"), "ffn2_w1": g("ffn2_w1"), "ffn1_w2": g("ffn1_w2"), "ffn2_w2": g("ffn2_w2"),
        "w_in_ext": w_in_ext, "w_gate": g("w_gate"), "w_branch": g("w_branch"), "w_out": g("w_out"),
        "lnp": pp(np.stack([g("ln1_g"), g("ln1_b"), g("ln2_g"), g("ln2_b"), g("ln3_g"), g("ln3_b")], axis=1)),
        "b_gate": pp(g("b_gate")), "conv_w": pp(g("conv_w")), "pool_w": g("pool_w"),
        "vec256": vec256, "vec256p": pp(vec256),
        "sgu_wT": np.ascontiguousarray(np.transpose(g("sgu_w"), (0, 1, 3, 2))),
        "sgu_b": g("sgu_b"),
    }
    return shared


_NC_CACHE = {}
SAFE_STOP = None


def kernel(**inputs):
    x = np.asarray(inputs["x"], dtype=np.float32)
    shared = host_layout(inputs)
    tabs = [host_tables(0), host_tables(1)]
    in_maps = []
    for c in range(8):
        b, half = c // 2, c % 2
        m = dict(shared)
        m.update(tabs[half])
        m["xT"] = np.ascontiguousarray(x[b, half * T:(half + 1) * T, :].T)
        in_maps.append(m)
    if "nc" not in _NC_CACHE:
        _NC_CACHE["nc"] = build(stop=SAFE_STOP)
    res = run_bass_kernel_spmd(_NC_CACHE["nc"], in_maps, core_ids=list(range(8)))
    out = np.empty_like(x)
    for c in range(8):
        b, half = c // 2, c % 2
        out[b, half * T:(half + 1) * T, :] = res.results[c]["outT"].T
    return out
```
